# Optimizing a Trainium2 kernel written in Bass

```python
import jax, jax.numpy as jnp
from jax import lax
import numpy as np

D_MODEL = 1024
BATCH = 8
SEQ = 2048
DEPTH = 1
DEC_BATCH = 128
DEC_SEQ = 4
PAST_LEN = 16384
PAGE_SIZE = 128

N_META = 16
D_CONV = D_MODEL
CONV_A_WIDTH = 3
HEAD_K = 128
HEAD_V = 128
N_HEADS = D_MODEL // 128
QKV_DIM = 2 * N_HEADS * HEAD_K + N_HEADS * HEAD_V
CONV_QKV_WIDTH = 4
CHUNK = 64
EPS = 1e-6
IN_SIZES = (D_CONV, D_CONV, D_CONV, D_CONV, QKV_DIM, N_HEADS * HEAD_V, N_HEADS, N_HEADS, D_MODEL, D_MODEL)
N_IN = 4 * D_CONV + QKV_DIM + N_HEADS * HEAD_V + 2 * N_HEADS + 2 * D_MODEL

kernel_name = "hybrid_shortconv_gated_deltanet_step"


def _rmsnorm(x, w):
    xf = x.astype(jnp.float32)
    y = xf * lax.rsqrt(jnp.mean(xf * xf, axis=-1, keepdims=True) + EPS)
    return (y * w.astype(jnp.float32)).astype(x.dtype)


def _l2norm(x):
    return x * lax.rsqrt(jnp.sum(x * x, axis=-1, keepdims=True) + EPS)


def _causal_dwconv(x, prefix, w):
    width = w.shape[0]
    t_len = x.shape[1]
    xp = jnp.concatenate([prefix.astype(x.dtype), x], axis=1)
    y = xp[:, 0:t_len] * w[0]
    for j in range(1, width):
        y = y + xp[:, j:j + t_len] * w[j]
    return y, xp[:, t_len:]


def _delta_chunk(q, k, v, beta, g, s):
    L = q.shape[2]
    G = jnp.cumsum(g, axis=-1)
    diff = G[..., :, None] - G[..., None, :]
    incl = jnp.tril(jnp.ones((L, L), dtype=bool))
    strict = jnp.tril(jnp.ones((L, L), dtype=bool), -1)
    decay = jnp.exp(jnp.where(incl, diff, -jnp.inf))
    kb = k * beta[..., None]
    a = jnp.where(strict, jnp.einsum('bhid,bhjd->bhij', kb, k) * decay, 0.0)
    eye = jnp.eye(L, dtype=q.dtype)
    t_inv = lax.linalg.triangular_solve(eye + a, jnp.broadcast_to(eye, a.shape),
                                        left_side=True, lower=True, unit_diagonal=True)
    value = jnp.einsum('bhij,bhjv->bhiv', t_inv, v * beta[..., None])
    k_cum = jnp.einsum('bhij,bhjd->bhid', t_inv, kb * jnp.exp(G)[..., None])
    v_new = value - jnp.einsum('bhld,bhdv->bhlv', k_cum, s)
    attn = jnp.einsum('bhid,bhjd->bhij', q, k) * decay
    o = (jnp.einsum('bhld,bhdv->bhlv', q * jnp.exp(G)[..., None], s)
         + jnp.einsum('bhij,bhjv->bhiv', attn, v_new))
    g_last = G[..., -1]
    s_new = (s * jnp.exp(g_last)[..., None, None]
             + jnp.einsum('bhld,bhlv->bhdv', k * jnp.exp(g_last[..., None] - G)[..., None], v_new))
    return o, s_new


def _delta_sequence(q, k, v, beta, g, s):
    t_len = q.shape[2]
    n_full = t_len // CHUNK
    rem = t_len - n_full * CHUNK
    outs = []
    if n_full > 0:
        def to_chunks(a):
            a = a[:, :, :n_full * CHUNK]
            a = a.reshape(a.shape[:2] + (n_full, CHUNK) + a.shape[3:])
            return jnp.moveaxis(a, 2, 0)
        xs = tuple(to_chunks(a) for a in (q, k, v, beta, g))

        def step(carry, c):
            o_c, carry = _delta_chunk(c[0], c[1], c[2], c[3], c[4], carry)
            return carry, o_c
        s, o_all = lax.scan(step, s, xs)
        o_all = jnp.moveaxis(o_all, 0, 2)
        outs.append(o_all.reshape(o_all.shape[:2] + (n_full * CHUNK, o_all.shape[-1])))
    if rem > 0:
        st = n_full * CHUNK
        o_r, s = _delta_chunk(q[:, :, st:], k[:, :, st:], v[:, :, st:], beta[:, :, st:], g[:, :, st:], s)
        outs.append(o_r)
    o = outs[0] if len(outs) == 1 else jnp.concatenate(outs, axis=2)
    return o, s


def _layer(x, conv_a_prev, conv_qkv_prev, s_prev, n_meta, norm_pre, norm_post, w_in, b_gate,
           conv_a_w, conv_qkv_w, a_log, dt_bias, gnorm_w, w_a_out, w_b_out, w_o):
    f32 = jnp.float32
    bsz, t_len, _ = x.shape
    xn = _rmsnorm(x, norm_pre)
    proj = xn @ w_in
    splits = []
    acc = 0
    for sz in IN_SIZES[:-1]:
        acc += sz
        splits.append(acc)
    h_a, b_a, c_a, z_a, qkv, z_b, beta_l, alpha_l, gate_a, gate_b = jnp.split(proj, splits, axis=-1)

    conv_u, conv_a_new = _causal_dwconv(c_a * h_a, conv_a_prev, conv_a_w)
    y_a = b_a * conv_u * jax.nn.silu(z_a)

    qkv_c, conv_qkv_new = _causal_dwconv(qkv, conv_qkv_prev, conv_qkv_w)
    qkv_c = jax.nn.silu(qkv_c).astype(f32)
    nk = N_HEADS * HEAD_K

    def heads(a, d):
        return a.reshape(bsz, t_len, N_HEADS, d).transpose(0, 2, 1, 3)
    q = _l2norm(heads(qkv_c[..., :nk], HEAD_K)) * (HEAD_K ** -0.5)
    k = _l2norm(heads(qkv_c[..., nk:2 * nk], HEAD_K))
    v = heads(qkv_c[..., 2 * nk:], HEAD_V)
    beta = jax.nn.sigmoid(beta_l.astype(f32)).transpose(0, 2, 1)
    g = (-jnp.exp(a_log.astype(f32))
         * jax.nn.softplus(alpha_l.astype(f32) + dt_bias.astype(f32))).transpose(0, 2, 1)
    if n_meta > 0:
        o_m, s_mid = _delta_sequence(q[:, :, :n_meta], k[:, :, :n_meta], v[:, :, :n_meta],
                                     beta[:, :, :n_meta], g[:, :, :n_meta], s_prev)
        o_r, s_new = _delta_sequence(q[:, :, n_meta:], k[:, :, n_meta:], v[:, :, n_meta:],
                                     beta[:, :, n_meta:], g[:, :, n_meta:], s_mid)
        o = jnp.concatenate([o_m, o_r], axis=2)
    else:
        o, s_new = _delta_sequence(q, k, v, beta, g, s_prev)
    o = o.transpose(0, 2, 1, 3).astype(x.dtype)
    o = _rmsnorm(o, gnorm_w) * jax.nn.silu(z_b.reshape(bsz, t_len, N_HEADS, HEAD_V))
    y_b = o.reshape(bsz, t_len, N_HEADS * HEAD_V)

    merged = (jax.nn.sigmoid(gate_a + b_gate[:D_MODEL]) * (y_a @ w_a_out)
              + jax.nn.sigmoid(gate_b + b_gate[D_MODEL:]) * (y_b @ w_b_out))
    x = x + _rmsnorm(merged @ w_o, norm_post)
    return x, conv_a_new, conv_qkv_new, s_new.astype(x.dtype)


def setup_inputs(seed: int = 0) -> dict:
    key = jax.random.key(seed)
    ks = jax.random.split(key, 18)
    f32 = jnp.float32

    def nrm(k, shape, scale):
        return jax.random.normal(k, shape, f32) * scale
    x_prompt = nrm(ks[0], (BATCH, SEQ, D_MODEL), 1.0)
    x_sample = nrm(ks[1], (DEC_BATCH, DEC_SEQ, D_MODEL), 1.0)
    state_conv_a = nrm(ks[2], (DEPTH, DEC_BATCH, CONV_A_WIDTH - 1, D_CONV), 1.0)
    state_conv_qkv = nrm(ks[3], (DEPTH, DEC_BATCH, CONV_QKV_WIDTH - 1, QKV_DIM), 1.0)
    state_delta = nrm(ks[4], (DEPTH, DEC_BATCH, N_HEADS, HEAD_K, HEAD_V), HEAD_K ** -0.5)
    meta = nrm(ks[5], (N_META, D_MODEL), 1.0)
    norm_pre = 1.0 + nrm(ks[6], (DEPTH, D_MODEL), 0.02)
    norm_post = 1.0 + nrm(ks[7], (DEPTH, D_MODEL), 0.02)
    w_in = nrm(ks[8], (DEPTH, D_MODEL, N_IN), D_MODEL ** -0.5)
    b_gate = nrm(ks[9], (DEPTH, 2 * D_MODEL), 0.01)
    conv_a_w = nrm(ks[10], (DEPTH, CONV_A_WIDTH, D_CONV), CONV_A_WIDTH ** -0.5)
    conv_qkv_w = nrm(ks[11], (DEPTH, CONV_QKV_WIDTH, QKV_DIM), CONV_QKV_WIDTH ** -0.5)
    a_log = jnp.log(jax.random.uniform(ks[12], (DEPTH, N_HEADS), f32, 1.0, 16.0))
    dt = jax.random.uniform(ks[13], (DEPTH, N_HEADS), f32, 1e-3, 1e-1)
    dt_bias = jnp.log(jnp.expm1(dt))
    gnorm_w = 1.0 + nrm(ks[14], (DEPTH, HEAD_V), 0.02)
    w_a_out = nrm(ks[15], (DEPTH, D_CONV, D_MODEL), D_CONV ** -0.5)
    w_b_out = nrm(ks[16], (DEPTH, N_HEADS * HEAD_V, D_MODEL), (N_HEADS * HEAD_V) ** -0.5)
    w_o = nrm(ks[17], (DEPTH, D_MODEL, D_MODEL), D_MODEL ** -0.5)
    return {"x_prompt": x_prompt, "x_sample": x_sample,
            "state_conv_a": state_conv_a, "state_conv_qkv": state_conv_qkv, "state_delta": state_delta,
            "meta": meta, "norm_pre": norm_pre, "norm_post": norm_post, "w_in": w_in, "b_gate": b_gate,
            "conv_a_w": conv_a_w, "conv_qkv_w": conv_qkv_w, "a_log": a_log, "dt_bias": dt_bias,
            "gnorm_w": gnorm_w, "w_a_out": w_a_out, "w_b_out": w_b_out, "w_o": w_o}


def reference(x_prompt, x_sample, state_conv_a, state_conv_qkv, state_delta, meta, norm_pre, norm_post,
              w_in, b_gate, conv_a_w, conv_qkv_w, a_log, dt_bias, gnorm_w, w_a_out, w_b_out, w_o):
    dtype = x_prompt.dtype
    bsz = x_prompt.shape[0]
    xp = jnp.concatenate([jnp.broadcast_to(meta[None].astype(dtype), (bsz, N_META, D_MODEL)), x_prompt], axis=1)
    xs = x_sample
    ca_p, cq_p, sd_p, ca_s, cq_s, sd_s = [], [], [], [], [], []
    for l in range(DEPTH):
        w = (norm_pre[l], norm_post[l], w_in[l], b_gate[l], conv_a_w[l], conv_qkv_w[l],
             a_log[l], dt_bias[l], gnorm_w[l], w_a_out[l], w_b_out[l], w_o[l])
        zero_a = jnp.zeros((bsz, CONV_A_WIDTH - 1, D_CONV), dtype)
        zero_q = jnp.zeros((bsz, CONV_QKV_WIDTH - 1, QKV_DIM), dtype)
        zero_s = jnp.zeros((bsz, N_HEADS, HEAD_K, HEAD_V), jnp.float32)
        xp, ca, cq, sd = _layer(xp, zero_a, zero_q, zero_s, N_META, *w)
        ca_p.append(ca)
        cq_p.append(cq)
        sd_p.append(sd)
        xs, ca, cq, sd = _layer(xs, state_conv_a[l], state_conv_qkv[l],
                                state_delta[l].astype(jnp.float32), 0, *w)
        ca_s.append(ca)
        cq_s.append(cq)
        sd_s.append(sd)
    y_prompt = xp[:, N_META:]
    return (y_prompt, xs, jnp.stack(ca_p), jnp.stack(cq_p), jnp.stack(sd_p),
            jnp.stack(ca_s), jnp.stack(cq_s), jnp.stack(sd_s))
```

```python
import os
import numpy as np
from contextlib import ExitStack
import concourse.bass as bass
import concourse.mybir as mybir
from concourse.bass_utils import run_bass_kernel_spmd

F32 = mybir.dt.float32
BF16 = mybir.dt.bfloat16
AF = mybir.ActivationFunctionType
ALU = mybir.AluOpType

NCORES = 8
D = 1024
NP = 2064
NSQ = 16
NS = 64
NT = NP + NS
NIN = 10256
EPS = 1e-6
TBS = [(0, 512), (512, 1024), (1024, 1536), (1536, 2048), (2048, 2128)]
OFF_H, OFF_B, OFF_C, OFF_Z, OFF_QKV, OFF_ZB, OFF_BA, OFF_GA, OFF_GB = 0, 1024, 2048, 3072, 4096, 7168, 8192, 8208, 9232
ENG = ['pe', 'act', 'dve', 'pool', 'sp']
NDS = 24


class Sched:
    def __init__(self, nc, es):
        self.nc = nc
        self.q = {e: [] for e in ENG}
        self.sem = {e: es.enter_context(nc.semaphore("s_" + e)) for e in ENG}
        self.cnt = {e: 0 for e in ENG}
        self.seen = {e: {} for e in ENG}
        self.lastw = {}
        self.rd = {}
        self.dsem = [es.enter_context(nc.semaphore("d%d" % i)) for i in range(NDS)]
        self.dcnt = [0] * NDS
        self.dnext = 0
        self.nops = 0

    def _wait(self, e, toks):
        for tok in toks:
            kind, who, val = tok
            if kind == 'e' and who == e and e == 'pe':
                continue
            key = (kind, who)
            if self.seen[e].get(key, 0) >= val:
                continue
            self.seen[e][key] = val
            s = self.sem[who] if kind == 'e' else self.dsem[who]
            self.q[e].append(lambda eng, s=s, v=val: eng.wait_ge(s, v))

    @staticmethod
    def _excl(r, w):
        return list(w) + [k for k in r if isinstance(k, tuple) and k[0] == 'ps' and k not in w]

    def _deps(self, r, w):
        toks = []
        for k in r:
            toks += self.lastw.get(k, [])
        for k in w:
            toks += self.lastw.get(k, [])
            toks += self.rd.get(k, [])
        return toks

    def _record(self, tok, r, w):
        for k in r:
            lst = self.rd.setdefault(k, [])
            lst[:] = [t for t in lst if not (t[0] == tok[0] and t[1] == tok[1])]
            lst.append(tok)
        for k in w:
            self.lastw[k] = [tok]
            self.rd[k] = []

    def op(self, e, fn, r=(), w=()):
        w = self._excl(r, w)
        self._wait(e, self._deps(r, w))
        self.cnt[e] += 1
        s = self.sem[e]
        self.q[e].append(lambda eng, fn=fn, s=s: fn(eng).then_inc(s, 1))
        self._record(('e', e, self.cnt[e]), r, w)
        self.nops += 1

    def mm(self, fns, r=(), w=()):
        w = self._excl(r, w)
        self._wait('pe', self._deps(r, w))
        for fn in fns[:-1]:
            self.q['pe'].append(lambda eng, fn=fn: fn(eng))
        self.cnt['pe'] += 1
        s = self.sem['pe']
        self.q['pe'].append(lambda eng, fn=fns[-1], s=s: fn(eng).then_inc(s, 1))
        self._record(('e', 'pe', self.cnt['pe']), r, w)
        self.nops += len(fns)

    def dma(self, out, in_, r=(), w=(), e='sp', **kw):
        slot = self.dnext
        self.dnext = (self.dnext + 1) % NDS
        toks = self._deps(r, w)
        if self.dcnt[slot] > 0:
            toks.append(('d', slot, self.dcnt[slot]))
        self._wait(e, toks)
        self.dcnt[slot] += 16
        s = self.dsem[slot]
        self.q[e].append(lambda eng, o=out, i=in_, s=s, kw=kw: eng.dma_start(out=o, in_=i, **kw).then_inc(s, 16))
        self._record(('d', slot, self.dcnt[slot]), r, w)

    def barrier(self):
        toks = [('e', e, self.cnt[e]) for e in ENG if self.cnt[e] > 0]
        toks += [('d', i, self.dcnt[i]) for i in range(NDS) if self.dcnt[i] > 0]
        for e in ENG:
            self._wait(e, [t for t in toks if not (t[0] == 'e' and t[1] == e)])

    def finish(self):
        toks = [('d', i, self.dcnt[i]) for i in range(NDS) if self.dcnt[i] > 0]
        toks += [('e', e, self.cnt[e]) for e in ENG if self.cnt[e] > 0]
        self._wait('sp', toks)

    def emit(self, block):
        q = self.q

        @block.sync
        def _(eng):
            for f in q['sp']:
                f(eng)

        @block.scalar
        def _(eng):
            for f in q['act']:
                f(eng)

        @block.vector
        def _(eng):
            for f in q['dve']:
                f(eng)

        @block.gpsimd
        def _(eng):
            for f in q['pool']:
                f(eng)

        @block.tensor
        def _(eng):
            for f in q['pe']:
                f(eng)


def _chunks():
    ch = [(0, 16, 0)]
    for n in range(32):
        ch.append((16 + 64 * n, 64, 0))
    for b in range(NSQ):
        ch.append((NP + 4 * b, 4, 1 + b))
    return ch


def build(upto=99, debug=False):
    nc = bass.Bass("TRN2", target_bir_lowering=False)
    es = ExitStack()

    def din(name, shape, dt=F32):
        return nc.dram_tensor(name, list(shape), dt, kind="ExternalInput").ap()

    def dout(name, shape, dt=F32):
        return nc.dram_tensor(name, list(shape), dt, kind="ExternalOutput").ap()

    x_all = din("x_all", [NT, D])
    w_in = din("w_in", [D, NIN])
    w_a_out = din("w_a_out", [D, D])
    w_b_out = din("w_b_out", [D, D])
    w_o = din("w_o", [D, D])
    scq = din("scq", [128, 24, NSQ, 3])
    sca = din("sca", [128, 8, NSQ, 2])
    s0 = din("s0", [NSQ, 8, 128, 128])
    cwq_d = din("cwq", [128, 24 * 4])
    cwa_d = din("cwa", [128, 8 * 3])
    bgate_d = din("bgate", [128, 16])
    gnorm_d = din("gnorm", [128, 1])
    npre_d = din("npre_bc", [128, D])
    npost_d = din("npost_bc", [128, D])
    alog_d = din("alog_bc", [128, 8])
    dtb_d = din("dtb_bc", [128, 8])

    y_out = dout("y_out", [NT - 16, D])
    ocq = dout("ocq", [128, 24, 1 + NSQ, 3])
    oca = dout("oca", [128, 8, 1 + NSQ, 2])
    s_out = dout("s_out", [1 + NSQ, 8, 128, 128])
    dbg = {}

    def sb(name, shape, dt=F32):
        return es.enter_context(nc.sbuf_tensor(name, list(shape), dt))

    S = Sched(nc, es)
    ph = [ExitStack()]

    def sbl(name, shape, dt=F32):
        return ph[0].enter_context(nc.sbuf_tensor(name, list(shape), dt))

    def end_phase():
        S.barrier()
        ph[0].close()
        ph[0] = ExitStack()

    xnT = sb("xnT", [128, 8, NT], BF16)
    qk = sb("qk", [128, 16, NT], BF16)
    vT = sb("vT", [128, 8, NT], BF16)
    identf = sb("identf", [128, 128], F32)
    identb = sb("identb", [128, 128], BF16)
    onesb = sb("onesb", [128, 128], BF16)
    cwq = sb("cwq_s", [128, 24 * 4], F32)
    cwa = sb("cwa_s", [128, 8 * 3], F32)
    bgate = sb("bgate_s", [128, 16], F32)
    gnorm = sb("gnorm_s", [128, 1], F32)
    negA = sb("negA", [128, 8], F32)
    dtb = sb("dtb_s", [128, 8], F32)
    wba = sb("wba", [128, 8, 16], BF16)
    wbaf = sbl("wbaf", [128, 8, 16], F32)
    ps = [es.enter_context(nc.psum_tensor("ps%d" % i, [128, 1024], F32)) for i in range(4)]
    psn = [0]

    def bank():
        i = psn[0] % 8
        psn[0] += 1
        return ps[i // 2][:, (i % 2) * 512:(i % 2) * 512 + 512], ('ps', i)

    def dbank():
        if psn[0] % 2:
            psn[0] += 1
        i = psn[0] % 8
        psn[0] += 2
        return ps[i // 2][:, :], [('ps', i), ('ps', i + 1)]

    for t, k in ((cwq, cwq_d), (cwa, cwa_d), (bgate, bgate_d), (gnorm, gnorm_d),
                 (negA, alog_d), (dtb, dtb_d)):
        S.dma(t[:], k, w=[t.name])
    S.op('act', lambda e: e.activation(out=negA[:], in_=negA[:], func=AF.Exp), r=['negA'], w=['negA'])
    S.op('dve', lambda e: e.tensor_scalar(out=negA[:], in0=negA[:], scalar1=-1.0, scalar2=None, op0=ALU.mult),
         r=['negA'], w=['negA'])
    S.op('pool', lambda e: e.memset(identf[:], 0.0), w=['identf'])
    S.op('pool', lambda e: e.affine_select(out=identf[:], in_=identf[:], pattern=[[-1, 128]], compare_op=ALU.not_equal,
                                           fill=1.0, base=0, channel_multiplier=1), r=['identf'], w=['identf'])
    S.op('pool', lambda e: e.tensor_copy(out=identb[:], in_=identf[:]), r=['identf'], w=['identb'])
    S.op('pool', lambda e: e.memset(onesb[:], 1.0), w=['onesb'])
    S.dma(wbaf[:], w_in[:, OFF_BA:OFF_BA + 16].rearrange("(kc p) f -> p kc f", p=128), w=['wbaf'])
    S.op('pool', lambda e: e.tensor_copy(out=wba[:], in_=wbaf[:]), r=['wbaf'], w=['wba'])

    wst = []
    wgen = [0]

    class WStream:
        def __init__(self, reqs, nst, nbf, depth):
            wgen[0] += 1
            g = wgen[0]
            self.st = [sbl("wst%d_%d" % (i, g), [128, 8, 128], F32) for i in range(nst)]
            self.bf = [sbl("wbf%d_%d" % (i, g), [128, 8, 128], BF16) for i in range(nbf)]
            wst[:] = self.st
            self.reqs, self.depth, self.issued = reqs, depth, 0

        def _issue(self, i):
            src, f0 = self.reqs[i]
            st, bf = self.st[i % len(self.st)], self.bf[i % len(self.bf)]
            S.dma(st[:], src[:, f0:f0 + 128].rearrange("(kc p) f -> p kc f", p=128), w=[st.name])
            S.op('pool', lambda e: e.tensor_copy(out=bf[:], in_=st[:]), r=[st.name], w=[bf.name])

        def get(self, i):
            while self.issued < min(len(self.reqs), i + self.depth + 1):
                self._issue(self.issued)
                self.issued += 1
            bf = self.bf[i % len(self.bf)]
            return bf, bf.name

    def proj(wt, wkey, src, skey, t0, t1, extra_r=()):
        pb, pk = bank()
        n = t1 - t0
        fns = []
        for kc in range(8):
            fns.append(lambda e, kc=kc: e.matmul(pb[:, 0:n], lhsT=wt[:, kc, :], rhs=src[:, kc, t0:t1],
                                                 start=(kc == 0), stop=(kc == 7)))
        S.mm(fns, r=[wkey] + ([skey] if skey else []) + list(extra_r), w=[pk])
        return pb, pk

    npre = sbl("npre_s", [128, D], F32)
    S.dma(npre[:], npre_d, w=['npre_s'])
    xt = [sbl("xt%d" % i, [128, D], F32) for i in range(3)]
    xnf = [sbl("xnf%d" % i, [128, D], F32) for i in range(3)]
    junk = sbl("junk", [128, D], BF16)
    ssq = [sbl("ssq%d" % i, [128, 1], F32) for i in range(3)]
    tiles = [(i * 128, min(128, NT - i * 128)) for i in range((NT + 127) // 128)]
    for ti, (r0, n) in enumerate(tiles):
        x_, xn_, sq_ = xt[ti % 3], xnf[ti % 3], ssq[ti % 3]
        S.dma(x_[0:n, :], x_all[r0:r0 + n, :], w=[x_.name])
        S.op('act', lambda e, x_=x_, sq_=sq_, n=n: e.activation(out=junk[0:n, :], in_=x_[0:n, :], func=AF.Square,
                                                                 accum_out=sq_[0:n, :]),
             r=[x_.name], w=['junk', sq_.name])
        S.op('act', lambda e, sq_=sq_, n=n: e.activation(out=sq_[0:n, :], in_=sq_[0:n, :], func=AF.Sqrt,
                                                          scale=1.0 / D, bias=EPS), r=[sq_.name], w=[sq_.name])
        S.op('dve', lambda e, sq_=sq_, n=n: e.reciprocal(out=sq_[0:n, :], in_=sq_[0:n, :]), r=[sq_.name], w=[sq_.name])
        S.op('dve', lambda e, x_=x_, xn_=xn_, sq_=sq_, n=n: e.scalar_tensor_tensor(
            out=xn_[0:n, :], in0=x_[0:n, :], scalar=sq_[0:n, 0:1], in1=npre[0:n, :], op0=ALU.mult, op1=ALU.mult),
            r=[x_.name, sq_.name, 'npre_s'], w=[xn_.name])
        pd, pks = dbank()
        fns = []
        for kc in range(8):
            fns.append(lambda e, kc=kc, xn_=xn_, n=n, pd=pd: e.transpose(pd[:, kc * 128:kc * 128 + n],
                                                                          xn_[0:n, kc * 128:(kc + 1) * 128],
                                                                          identf[0:n, 0:n]))
        S.mm(fns, r=[xn_.name, 'identf'], w=pks)
        src = pd.rearrange("p (kc t) -> p kc t", kc=8)[:, :, 0:n]
        S.op('act' if ti % 2 else 'dve',
             (lambda e, src=src, r0=r0, n=n: e.activation(out=xnT[:, :, r0:r0 + n], in_=src, func=AF.Copy)) if ti % 2 else
             (lambda e, src=src, r0=r0, n=n: e.tensor_copy(out=xnT[:, :, r0:r0 + n], in_=src)),
             r=pks, w=['xnT'])

    if debug:
        dbg['xnT'] = dout("dbg_xnT", [128, 8, NT], BF16)
        S.dma(dbg['xnT'], xnT[:], r=['xnT'])
    end_phase()

    if upto >= 1:
        ws = WStream([(w_in, OFF_QKV + qc * 128) for qc in range(24)], 3, 4, 2)
        pend = []
        xq = [sbl("xq%d" % i, [128, 3 + NP], F32) for i in range(2)]
        xs = [sbl("xs%d" % i, [128, NSQ, 7], F32) for i in range(2)]
        accs_ = [sbl("acc%d" % i, [128, NT], F32) for i in range(2)]
        sqbs_ = [sbl("sqb%d" % i, [128, NT], BF16) for i in range(2)]
        lnbs_ = [sbl("lnb%d" % i, [128, 512], F32) for i in range(2)]
        for i in range(2):
            S.op('pool', lambda e, i=i: e.memset(xq[i][:, 0:3], 0.0), w=[(xq[i].name, 0)])
        ctx = {}

        def a1_ctx(qc):
            c = dict(which=qc // 8, h=qc % 8, xq=xq[qc % 2], xs=xs[qc % 2], acc=accs_[qc % 2], sqb=sqbs_[qc % 2])
            c["wt"], c["wk"] = ws.get(qc)
            for f in pend:
                f()
            pend.clear()
            S.dma(c["xs"][:, :, 0:3], scq[:, qc, :, :], w=[c["xs"].name])
            ctx[qc] = c
            return c

        def stageA(qc, ti):
            c = ctx[qc] if ti > 0 else a1_ctx(qc)
            t0, t1 = TBS[ti]
            xq_, xs_ = c["xq"], c["xs"]
            pb, pk = proj(c["wt"], c["wk"], xnT, 'xnT', t0, t1)
            npr = min(t1, NP) - t0
            S.op('act', lambda e: e.activation(out=xq_[:, 3 + t0:3 + t0 + npr], in_=pb[:, 0:npr], func=AF.Copy),
                 r=[pk], w=[(xq_.name, ti)])
            if t1 > NP:
                S.op('act', lambda e: e.activation(
                    out=xs_[:, :, 3:7], in_=pb[:, npr:npr + NS].rearrange("p (b t) -> p b t", t=4), func=AF.Copy),
                    r=[pk], w=[xs_.name])
                pend.append(lambda: S.dma(ocq[:, qc, 0, :], xq_[:, NP:NP + 3], r=[(xq_.name, 4)]))
                pend.append(lambda: S.dma(ocq[:, qc, 1:1 + NSQ, :], xs_[:, :, 4:7], r=[xs_.name]))

        def stageB(qc, ti):
            c = ctx[qc]
            t0, t1 = TBS[ti]
            xq_, xs_, acc, sqb, which, h = c["xq"], c["xs"], c["acc"], c["sqb"], c["which"], c["h"]
            npr = min(t1, NP) - t0
            cw = lambda j: cwq[:, qc * 4 + j:qc * 4 + j + 1]
            ak = (acc.name, ti)
            rk = [(xq_.name, ti)] + ([(xq_.name, ti - 1)] if ti > 0 else [])
            S.op('pool', lambda e: e.tensor_scalar(out=acc[:, t0:t0 + npr], in0=xq_[:, t0:t0 + npr], scalar1=cw(0),
                                                   scalar2=0.0, op0=ALU.mult, op1=ALU.add), r=rk + ['cwq_s'], w=[ak])
            for j in (1, 2, 3):
                S.op('dve', lambda e, j=j: e.scalar_tensor_tensor(
                    out=acc[:, t0:t0 + npr], in0=xq_[:, t0 + j:t0 + j + npr], scalar=cw(j), in1=acc[:, t0:t0 + npr],
                    op0=ALU.mult, op1=ALU.add), r=rk + [ak], w=[ak])
            if t1 > NP:
                accs = acc[:, NP:NT].rearrange("p (b t) -> p b t", t=4)
                S.op('pool', lambda e: e.tensor_scalar(out=accs, in0=xs_[:, :, 0:4], scalar1=cw(0), scalar2=0.0,
                                                       op0=ALU.mult, op1=ALU.add), r=[xs_.name, ak], w=[ak])
                for j in (1, 2, 3):
                    S.op('dve', lambda e, j=j: e.scalar_tensor_tensor(
                        out=accs, in0=xs_[:, :, j:j + 4], scalar=cw(j), in1=accs, op0=ALU.mult, op1=ALU.add),
                        r=[xs_.name, ak], w=[ak])
            if which == 2:
                S.op('act', lambda e: e.activation(out=vT[:, h, t0:t1], in_=acc[:, t0:t1], func=AF.Silu), r=[ak],
                     w=[('vT', h)])
                return
            S.op('act', lambda e: e.activation(out=acc[:, t0:t1], in_=acc[:, t0:t1], func=AF.Silu), r=[ak], w=[ak])
            S.op('act', lambda e: e.activation(out=sqb[:, t0:t1], in_=acc[:, t0:t1], func=AF.Square), r=[ak],
                 w=[(sqb.name, ti)])

        def stageC(qc):
            c = ctx[qc]
            acc, sqb, which = c["acc"], c["sqb"], c["which"]
            lnscale = float(np.log(128.0 ** -0.5)) if which == 0 else 0.0

            def one(ti, t0, t1):
                n = t1 - t0
                lnb = lnbs_[ti % 2]
                pb2, pk2 = bank()
                S.mm([lambda e: e.matmul(pb2[:, 0:n], lhsT=onesb[:], rhs=sqb[:, t0:t1], start=True, stop=True)],
                     r=[(sqb.name, ti), 'onesb'], w=[pk2])
                S.op('act', lambda e: e.activation(out=lnb[:, 0:n], in_=pb2[:, 0:n], func=AF.Ln, bias=EPS), r=[pk2],
                     w=[lnb.name])
                S.op('act', lambda e: e.activation(out=lnb[:, 0:n], in_=lnb[:, 0:n], func=AF.Exp, scale=-0.5, bias=lnscale),
                     r=[lnb.name], w=[lnb.name])
                S.op('dve', lambda e: e.tensor_tensor(out=qk[:, qc, t0:t1], in0=acc[:, t0:t1], in1=lnb[:, 0:n], op=ALU.mult),
                     r=[(acc.name, ti), lnb.name], w=[('qk', qc)])

            for ti, (t0, t1) in enumerate(TBS):
                one(ti, t0, t1)

        items = [(qc, ti) for qc in range(24) for ti in range(5)]
        SK = 2
        for i in range(len(items) + SK):
            if i < len(items):
                stageA(*items[i])
            j = i - SK
            if j >= 0:
                stageB(*items[j])
                if items[j][1] == 4 and items[j][0] < 16:
                    stageC(items[j][0])
        for f in pend:
            f()
        pend.clear()
        if debug:
            dbg['qk'] = dout("dbg_qk", [128, 16, NT], BF16)
            S.dma(dbg['qk'], qk[:], r=[('qk', i) for i in range(16)])
            dbg['vT'] = dout("dbg_vT", [128, 8, NT], BF16)
            S.dma(dbg['vT'], vT[:], r=[('vT', i) for i in range(8)])
        end_phase()

    if upto >= 2:
        Sf = sb("b_Sf", [128, 8, 128], F32)
        Sb = sb("b_Sb", [128, 8, 128], BF16)
        vTk = [('vT', h) for h in range(8)]
        qkk = [('qk', i) for i in range(16)]

        def old_path(cidx, tag):
            onesf = sbl("onesf" + tag, [64, 128], F32)
            triU = sbl("triU" + tag, [64, 64], F32)
            S.op('pool', lambda e: e.memset(onesf[:], 1.0), w=['onesf'])
            S.op('pool', lambda e: e.memset(triU[:], 1.0), w=['triU'])
            S.op('pool', lambda e: e.affine_select(out=triU[:], in_=triU[:], pattern=[[1, 64]], compare_op=ALU.is_ge,
                                                   fill=0.0, base=0, channel_multiplier=-1), r=['triU'], w=['triU'])
            maskU8 = sbl("maskU8" + tag, [64, 8, 64], F32)
            maskL8 = sbl("maskL8" + tag, [64, 8, 64], F32)
            ident8 = sbl("ident8" + tag, [64, 8, 64], F32)
            S.op('pool', lambda e: e.memset(maskU8[:], 1.0), w=['maskU8'])
            S.op('pool', lambda e: e.affine_select(out=maskU8[:], in_=maskU8[:], pattern=[[0, 8], [1, 64]],
                                                   compare_op=ALU.is_ge, fill=0.0, base=0, channel_multiplier=-1),
                 r=['maskU8'], w=['maskU8'])
            S.op('pool', lambda e: e.memset(maskL8[:], 1.0), w=['maskL8'])
            S.op('pool', lambda e: e.affine_select(out=maskL8[:], in_=maskL8[:], pattern=[[0, 8], [-1, 64]],
                                                   compare_op=ALU.is_gt, fill=0.0, base=0, channel_multiplier=1),
                 r=['maskL8'], w=['maskL8'])
            S.op('pool', lambda e: e.memset(ident8[:], 0.0), w=['ident8'])
            S.op('pool', lambda e: e.affine_select(out=ident8[:], in_=ident8[:], pattern=[[0, 8], [-1, 64]],
                                                   compare_op=ALU.not_equal, fill=1.0, base=0, channel_multiplier=1),
                 r=['ident8'], w=['ident8'])
            sm = {n: sbl("b%s_" % tag + n, [64, 8], F32) for n in ("bet", "nbet", "g", "G", "kdw", "eG", "cvec")}
            egl = sbl("b%s_" % tag + "egl", [128, 8], F32)
            fl = {n: sbl("b%s_" % tag + n, [64, 512], F32) for n in ("Dm", "nad", "t1", "t2", "N0", "N1", "M0", "M1", "Pm")}
            attnT = sbl("b%s_" % tag + "attnT", [64, 512], BF16)
            TT = sbl("b%s_" % tag + "TT", [64, 512], BF16)
            EG = sbl("b%s_" % tag + "EG", [128, 512], F32)
            qg = sbl("b%s_" % tag + "qg", [128, 512], BF16)
            kdec = sbl("b%s_" % tag + "kdec", [64, 1024], BF16)
            vb = sbl("b%s_" % tag + "vb", [64, 1024], BF16)
            rhs2 = sbl("b%s_" % tag + "rhs2", [64, 1024], BF16)
            vn = sbl("b%s_" % tag + "vn", [64, 1024], BF16)

            def v3(t, L, P=None):
                P = L if P is None else P
                return t[0:P, 0:8 * L].rearrange("p (h n) -> p h n", h=8)

            def w3(t, L):
                return t[0:L, :].rearrange("p (h n) -> p h n", h=8)

            def bc(t, L, n, P=None):
                P = L if P is None else P
                return t[0:P, :].unsqueeze(2).to_broadcast([P, 8, n])

            vTk = [('vT', h) for h in range(8)]
            qkk = [('qk', i) for i in range(16)]
            chunks = _chunks()
            STOP = int(os.environ.get("DBG_STOP", "99"))

            def do_chunk(ci, c0, L, seq):
                first = (ci == 0) or (chunks[ci - 1][2] != seq)
                last = (ci == len(chunks) - 1) or (chunks[ci + 1][2] != seq)
                nlev = {64: 5, 16: 3, 4: 1}[L]
                cs = slice(c0, c0 + L)
                if first:
                    if seq == 0:
                        S.op('pool', lambda e: e.memset(Sf[:], 0.0), w=['Sf'])
                        S.op('pool', lambda e: e.memset(Sb[:], 0.0), w=['Sb'])
                    else:
                        S.dma(Sf[:], s0[seq - 1].rearrange("h k v -> k h v"), w=['Sf'])
                        S.op('act', lambda e: e.activation(out=Sb[:], in_=Sf[:], func=AF.Copy), r=['Sf'], w=['Sb'])
                bet, nbet, g, G, kdw, eG, cvec = (sm[n] for n in ("bet", "nbet", "g", "G", "kdw", "eG", "cvec"))
                pb, pk = bank()
                S.mm([lambda e, kc=kc, pb=pb: e.matmul(pb[0:L, 0:16], lhsT=xnT[:, kc, cs], rhs=wba[:, kc, :],
                                                       start=(kc == 0), stop=(kc == 7)) for kc in range(8)],
                     r=['xnT', 'wba'], w=[pk])
                S.op('act', lambda e, pb=pb: e.activation(out=bet[0:L, :], in_=pb[0:L, 0:8], func=AF.Sigmoid), r=[pk], w=['bet'])
                S.op('dve', lambda e, pb=pb: e.tensor_tensor(out=g[0:L, :], in0=pb[0:L, 8:16], in1=dtb[0:L, :], op=ALU.add),
                     r=[pk, 'dtb_s'], w=['g'])
                S.op('act', lambda e: e.activation(out=g[0:L, :], in_=g[0:L, :], func=AF.Exp), r=['g'], w=['g'])
                S.op('act', lambda e: e.activation(out=g[0:L, :], in_=g[0:L, :], func=AF.Ln, bias=1.0), r=['g'], w=['g'])
                S.op('dve', lambda e: e.tensor_tensor(out=g[0:L, :], in0=g[0:L, :], in1=negA[0:L, :], op=ALU.mult),
                     r=['g', 'negA'], w=['g'])
                S.op('dve', lambda e: e.tensor_scalar(out=nbet[0:L, :], in0=bet[0:L, :], scalar1=-1.0, scalar2=None,
                                                      op0=ALU.mult), r=['bet'], w=['nbet'])
                if STOP < 3:
                    return
                pb2, pk2 = bank()
                S.mm([lambda e, pb2=pb2: e.matmul(pb2[0:L, 0:8], lhsT=triU[0:L, 0:L], rhs=g[0:L, :], start=True, stop=True),
                      lambda e, pb2=pb2: e.matmul(pb2[:, 8:16], lhsT=onesf[0:L, :], rhs=g[0:L, :], start=True, stop=True)],
                     r=['g', 'triU', 'onesf'], w=[pk2])
                S.op('dve', lambda e, pb2=pb2: e.tensor_copy(out=G[0:L, :], in_=pb2[0:L, 0:8]), r=[pk2], w=['G'])
                S.op('act', lambda e, pb2=pb2: e.activation(out=egl[:], in_=pb2[:, 8:16], func=AF.Exp), r=[pk2], w=['egl'])
                S.op('dve', lambda e, pb2=pb2: e.tensor_tensor(out=kdw[0:L, :], in0=pb2[0:L, 8:16], in1=G[0:L, :],
                                                               op=ALU.subtract), r=[pk2, 'G'], w=['kdw'])
                S.op('act', lambda e: e.activation(out=kdw[0:L, :], in_=kdw[0:L, :], func=AF.Exp), r=['kdw'], w=['kdw'])
                S.op('act', lambda e: e.activation(out=eG[0:L, :], in_=G[0:L, :], func=AF.Exp), r=['G'], w=['eG'])
                S.op('dve', lambda e: e.tensor_tensor(out=cvec[0:L, :], in0=nbet[0:L, :], in1=eG[0:L, :], op=ALU.mult),
                     r=['nbet', 'eG'], w=['cvec'])
                if STOP < 4:
                    return
                Dm, nad, t1, t2, Pm = fl["Dm"], fl["nad"], fl["t1"], fl["t2"], fl["Pm"]
                S.op('dve', lambda e: e.tensor_tensor(out=v3(Dm, L), in0=ident8[0:L, :, 0:L], in1=bc(G, L, L), op=ALU.mult),
                     r=['G', 'ident8'], w=['Dm'])
                pbG, pkG = bank()
                S.mm([lambda e, pbG=pbG: e.matmul(pbG[:, 0:8 * L], lhsT=onesf[0:L, :], rhs=Dm[0:L, 0:8 * L], start=True,
                                                  stop=True)], r=['Dm', 'onesf'], w=[pkG])
                if STOP < 5:
                    return
                S.op('dve', lambda e, pbG=pbG: e.tensor_tensor(out=v3(nad, L), in0=v3(pbG, L), in1=bc(G, L, L),
                                                               op=ALU.subtract), r=[pkG, 'G'], w=['nad'])
                S.op('act', lambda e: e.activation(out=nad[0:L, 0:8 * L], in_=nad[0:L, 0:8 * L], func=AF.Abs), r=['nad'],
                     w=['nad'])
                S.op('act', lambda e: e.activation(out=nad[0:L, 0:8 * L], in_=nad[0:L, 0:8 * L], func=AF.Exp, scale=-1.0),
                     r=['nad'], w=['nad'])
                S.op('act', lambda e, pbG=pbG: e.activation(out=EG[:, 0:8 * L], in_=pbG[:, 0:8 * L], func=AF.Exp), r=[pkG],
                     w=['EG'])
                S.op('pool', lambda e: e.tensor_tensor(out=v3(qg, L, 128), in0=qk[:, 0:8, cs], in1=v3(EG, L, 128),
                                                       op=ALU.mult), r=['EG'] + qkk[0:8], w=['qg'])
                if STOP < 6:
                    return
                pbK, pkK = bank()
                S.mm([lambda e, h=h, pbK=pbK: e.matmul(pbK[0:L, h * L:(h + 1) * L], lhsT=qk[:, 8 + h, cs], rhs=qk[:, 8 + h, cs],
                                                       start=True, stop=True) for h in range(8)], r=qkk[8:16], w=[pkK])
                pbQ, pkQ = bank()
                S.mm([lambda e, h=h, pbQ=pbQ: e.matmul(pbQ[0:L, h * L:(h + 1) * L], lhsT=qk[:, 8 + h, cs], rhs=qk[:, h, cs],
                                                       start=True, stop=True) for h in range(8)], r=qkk, w=[pkQ])
                if STOP < 7:
                    return
                N, M = [fl["N0"], fl["N1"]], [fl["M0"], fl["M1"]]
                S.op('dve', lambda e, pbK=pbK: e.tensor_tensor(out=v3(t1, L), in0=v3(pbK, L), in1=v3(nad, L), op=ALU.mult),
                     r=[pkK, 'nad'], w=['t1'])
                S.op('pool', lambda e: e.tensor_tensor(out=v3(t1, L), in0=v3(t1, L), in1=maskL8[0:L, :, 0:L], op=ALU.mult),
                     r=['t1', 'maskL8'], w=['t1'])
                S.op('dve', lambda e: e.tensor_tensor(out=v3(N[0], L), in0=v3(t1, L), in1=bc(nbet, L, L), op=ALU.mult),
                     r=['t1', 'nbet'], w=['N0'])
                S.op('dve', lambda e, pbQ=pbQ: e.tensor_tensor(out=v3(t2, L), in0=v3(pbQ, L), in1=v3(nad, L), op=ALU.mult),
                     r=[pkQ, 'nad'], w=['t2'])
                S.op('pool', lambda e: e.tensor_tensor(out=v3(attnT, L), in0=v3(t2, L), in1=maskU8[0:L, :, 0:L], op=ALU.mult),
                     r=['t2', 'maskU8'], w=['attnT'])
                if STOP < 8:
                    return
                pbM, pkM = bank()
                S.mm([lambda e, h=h, pbM=pbM: e.matmul(pbM[0:L, h * L:(h + 1) * L], lhsT=N[0][0:L, h * L:(h + 1) * L],
                                                       rhs=identf[0:L, 0:L], start=True, stop=True) for h in range(8)],
                     r=['N0', 'identf'], w=[pkM])
                S.op('act', lambda e, pbM=pbM: e.activation(out=M[0][0:L, 0:8 * L], in_=pbM[0:L, 0:8 * L], func=AF.Copy),
                     r=[pkM], w=['M0'])
                S.op('dve', lambda e, pbM=pbM: e.tensor_tensor(out=v3(Pm, L), in0=v3(pbM, L), in1=ident8[0:L, :, 0:L],
                                                               op=ALU.add), r=[pkM, 'ident8'], w=['Pm'])
                if STOP < 9:
                    return
                cur = 0
                for lev in range(nlev):
                    nx = 1 - cur
                    Nc, Mc, Nn, Mn = N[cur], M[cur], N[nx], M[nx]
                    kNc, kMc, kNn, kMn = 'N%d' % cur, 'M%d' % cur, 'N%d' % nx, 'M%d' % nx
                    lastlev = lev == nlev - 1
                    pbN2, pkN2 = bank()
                    S.mm([lambda e, h=h, pbN2=pbN2, Nc=Nc, Mc=Mc: e.matmul(
                        pbN2[0:L, h * L:(h + 1) * L], lhsT=Mc[0:L, h * L:(h + 1) * L], rhs=Nc[0:L, h * L:(h + 1) * L],
                        start=True, stop=True) for h in range(8)], r=[kNc, kMc], w=[pkN2])
                    S.op('act', lambda e, pbN2=pbN2, Nn=Nn: e.activation(out=Nn[0:L, 0:8 * L], in_=pbN2[0:L, 0:8 * L],
                                                                         func=AF.Copy), r=[pkN2], w=[kNn])
                    if not lastlev:
                        pbM2, pkM2 = bank()
                        S.mm([lambda e, h=h, pbM2=pbM2, Nc=Nc, Mc=Mc: e.matmul(
                            pbM2[0:L, h * L:(h + 1) * L], lhsT=Nc[0:L, h * L:(h + 1) * L], rhs=Mc[0:L, h * L:(h + 1) * L],
                            start=True, stop=True) for h in range(8)], r=[kNc, kMc], w=[pkM2])
                        S.op('dve', lambda e, pbM2=pbM2, Mn=Mn: e.tensor_copy(out=Mn[0:L, 0:8 * L], in_=pbM2[0:L, 0:8 * L]),
                             r=[pkM2], w=[kMn])
                    pbP, pkP = bank()
                    S.mm([lambda e, h=h, pbP=pbP, Nn=Nn: e.matmul(
                        pbP[0:L, h * L:(h + 1) * L], lhsT=Nn[0:L, h * L:(h + 1) * L], rhs=Pm[0:L, h * L:(h + 1) * L],
                        start=True, stop=True) for h in range(8)], r=[kNn, 'Pm'], w=[pkP])
                    S.op('dve', lambda e, pbP=pbP: e.tensor_tensor(out=Pm[0:L, 0:8 * L], in0=Pm[0:L, 0:8 * L],
                                                                   in1=pbP[0:L, 0:8 * L], op=ALU.add), r=[pkP, 'Pm'], w=['Pm'])
                    cur = nx
                S.op('act', lambda e: e.activation(out=TT[0:L, 0:8 * L], in_=Pm[0:L, 0:8 * L], func=AF.Copy), r=['Pm'],
                     w=['TT'])
                if STOP < 10:
                    return
                pbk, pkk = bank()
                pbkb = pbk.bitcast(BF16)
                S.mm([lambda e, h=h, pbkb=pbkb: e.transpose(pbkb[0:L, h * 128:(h + 1) * 128], qk[:, 8 + h, cs], identb[:, :])
                      for h in range(8)], r=qkk[8:16] + ['identb'], w=[pkk])
                S.op('dve', lambda e, pbkb=pbkb: e.tensor_tensor(out=w3(kdec, L), in0=w3(pbkb, L), in1=bc(kdw, L, 128),
                                                                 op=ALU.mult), r=[pkk, 'kdw'], w=['kdec'])
                pbv, pkv = bank()
                pbvb = pbv.bitcast(BF16)
                S.mm([lambda e, h=h, pbvb=pbvb: e.transpose(pbvb[0:L, h * 128:(h + 1) * 128], vT[:, h, cs], identb[:, :])
                      for h in range(8)], r=vTk + ['identb'], w=[pkv])
                S.op('dve', lambda e, pbvb=pbvb: e.tensor_tensor(out=w3(vb, L), in0=w3(pbvb, L), in1=bc(bet, L, 128),
                                                                 op=ALU.mult), r=[pkv, 'bet'], w=['vb'])
                if STOP < 11:
                    return
                pd1, pk1 = dbank()
                S.mm([lambda e, h=h, pd1=pd1: e.matmul(pd1[0:L, h * 128:(h + 1) * 128], lhsT=qk[:, 8 + h, cs], rhs=Sb[:, h, :],
                                                       start=True, stop=True) for h in range(8)], r=qkk[8:16] + ['Sb'], w=pk1)
                S.op('dve', lambda e, pd1=pd1: e.tensor_tensor(out=w3(rhs2, L), in0=w3(pd1, L), in1=bc(cvec, L, 128),
                                                               op=ALU.mult), r=pk1 + ['cvec'], w=['rhs2'])
                S.op('pool', lambda e: e.tensor_tensor(out=w3(rhs2, L), in0=w3(rhs2, L), in1=w3(vb, L), op=ALU.add),
                     r=['rhs2', 'vb'], w=['rhs2'])
                pd2, pk2_ = dbank()
                S.mm([lambda e, h=h, pd2=pd2: e.matmul(pd2[0:L, h * 128:(h + 1) * 128], lhsT=TT[0:L, h * L:(h + 1) * L],
                                                       rhs=rhs2[0:L, h * 128:(h + 1) * 128], start=True, stop=True)
                      for h in range(8)], r=['TT', 'rhs2'], w=pk2_)
                S.op('act', lambda e, pd2=pd2: e.activation(out=vn[0:L, :], in_=pd2[0:L, :], func=AF.Copy), r=pk2_, w=['vn'])
                pbo, pko = bank()
                fns = []
                for h in range(8):
                    fns.append(lambda e, h=h, pbo=pbo: e.matmul(pbo[:, h * L:(h + 1) * L], lhsT=Sb[:, h, :],
                                                                rhs=qg[:, h * L:(h + 1) * L], start=True, stop=False))
                    fns.append(lambda e, h=h, pbo=pbo: e.matmul(pbo[:, h * L:(h + 1) * L], lhsT=vn[0:L, h * 128:(h + 1) * 128],
                                                                rhs=attnT[0:L, h * L:(h + 1) * L], start=False, stop=True))
                S.mm(fns, r=['Sb', 'qg', 'vn', 'attnT'], w=[pko])
                S.op('act', lambda e, pbo=pbo: e.activation(out=vT[:, :, cs], in_=v3(pbo, L, 128), func=AF.Copy), r=[pko],
                     w=vTk)
                pd3, pk3 = dbank()
                S.mm([lambda e, h=h, pd3=pd3: e.matmul(pd3[:, h * 128:(h + 1) * 128], lhsT=kdec[0:L, h * 128:(h + 1) * 128],
                                                       rhs=vn[0:L, h * 128:(h + 1) * 128], start=True, stop=True)
                      for h in range(8)], r=['kdec', 'vn'], w=pk3)
                S.op('dve', lambda e: e.tensor_tensor(out=Sf[:], in0=Sf[:], in1=bc(egl, 128, 128, 128), op=ALU.mult),
                     r=['Sf', 'egl'], w=['Sf'])
                S.op('dve', lambda e, pd3=pd3: e.tensor_tensor(out=Sf[:], in0=Sf[:],
                                                               in1=pd3.rearrange("p (h n) -> p h n", h=8), op=ALU.add),
                     r=['Sf'] + pk3, w=['Sf'])
                S.op('act', lambda e: e.activation(out=Sb[:], in_=Sf[:], func=AF.Copy), r=['Sf'], w=['Sb'])
                if last:
                    S.dma(s_out[seq].rearrange("h k v -> k h v"), Sf[:], r=['Sf'])
                if debug and ci == int(os.environ.get("DBG_CHUNK", "0")):
                    for n, t in list(sm.items()) + list(fl.items()) + [("egl", egl), ("attnT", attnT), ("TT", TT), ("EG", EG),
                                                                       ("qg", qg), ("kdec", kdec), ("vb", vb), ("rhs2", rhs2),
                                                                       ("vn", vn), ("Sf", Sf), ("Sb", Sb)]:
                        dbg[n] = dout("dbgc_" + n, list(t.shape), t.dtype)
                        S.dma(dbg[n], t[:], r=[n])
            for ci in cidx:
                do_chunk(ci, *chunks[ci])
            end_phase()

        def new_pairs():
            F32R = mybir.dt.float32r
            NPAIR = int(os.environ.get("DBG_NPAIR", "16"))
            mU = sbl("n_mU", [128, 128], F32)
            mL = sbl("n_mL", [128, 128], F32)
            onesF = sbl("n_ones", [128, 128], F32)
            S.op('pool', lambda e: e.memset(onesF[:], 1.0), w=['n_ones'])
            for t, key in ((mU, 'n_mU'), (mL, 'n_mL')):
                S.op('pool', lambda e, t=t: e.memset(t[:], 1.0), w=[key])
                S.op('pool', lambda e, t=t: e.affine_select(out=t[:], in_=t[:], pattern=[[-64, 2], [0, 64]],
                                                            compare_op=ALU.is_ge, fill=0.0, base=0, channel_multiplier=1),
                     r=[key], w=[key])
                S.op('pool', lambda e, t=t: e.affine_select(out=t[:], in_=t[:], pattern=[[64, 2], [0, 64]],
                                                            compare_op=ALU.is_ge, fill=0.0, base=63, channel_multiplier=-1),
                     r=[key], w=[key])
            S.op('pool', lambda e: e.affine_select(out=mU[:], in_=mU[:], pattern=[[64, 2], [1, 64]], compare_op=ALU.is_ge,
                                                   fill=0.0, base=0, channel_multiplier=-1), r=['n_mU'], w=['n_mU'])
            S.op('pool', lambda e: e.affine_select(out=mL[:], in_=mL[:], pattern=[[-64, 2], [-1, 64]], compare_op=ALU.is_gt,
                                                   fill=0.0, base=0, channel_multiplier=1), r=['n_mL'], w=['n_mL'])
            kbgs = [sbl("n_kbg%d" % i, [128, 1024], BF16) for i in range(2)]
            vn = sbl("n_vn", [128, 1024], BF16)
            sms = []
            for i in range(4):
                d = {n: sbl("n_%s%d" % (n, i), [128, 8], F32) for n in ("bet", "nbet", "g", "G", "kdw", "cvec")}
                d["egl2"] = sbl("n_egl2%d" % i, [128, 16], F32)
                sms.append(d)
            Bs = []
            for i in range(2):
                d = {n: sbl("n_%s%d" % (n, i), [128, 1024], BF16) for n in ("TT", "attnT", "vb", "kdec", "nkcT", "qg")}
                Bs.append(d)
            wk = ExitStack()
            wkt = lambda name, shape, dt=F32: wk.enter_context(nc.sbuf_tensor(name, list(shape), dt))
            X1 = wkt("n_X1", [128, 1024])
            X2 = wkt("n_X2", [128, 1024])
            X4 = wkt("n_X4", [128, 1024])
            Nt = [wkt("n_Nt%d" % i, [128, 1024], BF16) for i in range(2)]
            MP = [wkt("n_MP%d" % i, [128, 8, 256], BF16) for i in range(2)]
            NtS = wkt("n_NtS", [128, 1024], BF16)
            MPS = wkt("n_MPS", [128, 8, 256], BF16)
            h3 = lambda ap: ap.rearrange("p (h n) -> p h n", h=8)
            bh = lambda t: t[:, :].unsqueeze(1).to_broadcast([128, 8, 128])
            bn = lambda ap: ap.unsqueeze(2).to_broadcast([128, 8, 128])
            Ntr = [t[:] for t in Nt]
            MPr = [t[:] for t in MP]
            identr_r = identb[:, :]

            def small(cs, sm):
                bet, nbet, g, G, kdw, cvec, egl2 = (sm[n] for n in ("bet", "nbet", "g", "G", "kdw", "cvec", "egl2"))
                k = lambda n: sm[n].name
                pb, pk = bank()
                S.mm([lambda e, kc=kc: e.matmul(pb[:, 0:16], lhsT=xnT[:, kc, cs], rhs=wba[:, kc, :], start=(kc == 0),
                                                stop=(kc == 7)) for kc in range(8)], r=['xnT', 'wba'], w=[pk])
                S.op('act', lambda e: e.activation(out=bet[:], in_=pb[:, 0:8], func=AF.Exp, scale=-1.0), r=[pk], w=[k("bet")])
                S.op('dve', lambda e: e.tensor_tensor(out=g[:], in0=pb[:, 8:16], in1=dtb[:], op=ALU.add), r=[pk, 'dtb_s'],
                     w=[k("g")])
                S.op('dve', lambda e: e.tensor_scalar(out=bet[:], in0=bet[:], scalar1=1.0, scalar2=None, op0=ALU.add),
                     r=[k("bet")], w=[k("bet")])
                S.op('dve', lambda e: e.reciprocal(out=bet[:], in_=bet[:]), r=[k("bet")], w=[k("bet")])
                S.op('dve', lambda e: e.tensor_scalar(out=nbet[:], in0=bet[:], scalar1=-1.0, scalar2=None, op0=ALU.mult),
                     r=[k("bet")], w=[k("nbet")])
                S.op('act', lambda e: e.activation(out=g[:], in_=g[:], func=AF.Exp), r=[k("g")], w=[k("g")])
                S.op('act', lambda e: e.activation(out=g[:], in_=g[:], func=AF.Ln, bias=1.0), r=[k("g")], w=[k("g")])
                S.op('dve', lambda e: e.tensor_tensor(out=g[:], in0=g[:], in1=negA[:], op=ALU.mult), r=[k("g"), 'negA'],
                     w=[k("g")])
                yield
                pbs, pks = bank()
                S.mm([lambda e: e.matmul(pbs[:, 0:8], lhsT=mU[:], rhs=g[:], start=True, stop=True),
                      lambda e: e.matmul(pbs[:, 8:16], lhsT=mU[:], rhs=g[:], start=True, stop=False),
                      lambda e: e.matmul(pbs[:, 8:16], lhsT=mL[:], rhs=g[:], start=False, stop=True),
                      lambda e: e.matmul(pbs[:, 16:24], lhsT=onesF[0:64, :], rhs=g[0:64, :], start=True, stop=True),
                      lambda e: e.matmul(pbs[:, 24:32], lhsT=onesF[:, :], rhs=g[:, :], start=True, stop=True)],
                     r=[k("g"), 'n_mU', 'n_mL', 'n_ones'], w=[pks])
                S.op('dve', lambda e: e.tensor_copy(out=G[:], in_=pbs[:, 0:8]), r=[pks], w=[k("G")])
                S.op('dve', lambda e: e.tensor_tensor(out=kdw[:], in0=pbs[:, 8:16], in1=G[:], op=ALU.subtract),
                     r=[pks, k("G")], w=[k("kdw")])
                S.op('dve', lambda e: e.tensor_copy(out=egl2[:], in_=pbs[:, 16:32]), r=[pks], w=[k("egl2")])
                S.op('dve', lambda e: e.tensor_tensor(out=egl2[:, 8:16], in0=egl2[:, 8:16], in1=egl2[:, 0:8],
                                                      op=ALU.subtract), r=[k("egl2")], w=[k("egl2")])
                S.op('act', lambda e: e.activation(out=egl2[:], in_=egl2[:], func=AF.Exp), r=[k("egl2")], w=[k("egl2")])
                S.op('act', lambda e: e.activation(out=kdw[:], in_=kdw[:], func=AF.Exp), r=[k("kdw")], w=[k("kdw")])
                S.op('act', lambda e: e.activation(out=cvec[:], in_=G[:], func=AF.Exp), r=[k("G")], w=[k("cvec")])
                S.op('dve', lambda e: e.tensor_tensor(out=cvec[:], in0=nbet[:], in1=cvec[:], op=ALU.mult),
                     r=[k("nbet"), k("cvec")], w=[k("cvec")])
                yield

            def setup(cs, B, sm, kbg):
                bet, nbet, G, kdw, cvec = (sm[n] for n in ("bet", "nbet", "G", "kdw", "cvec"))
                k = lambda n: sm[n].name
                bk = lambda n: B[n].name
                S.op('dve', lambda e: e.tensor_tensor(out=h3(X1[:]), in0=bh(identf), in1=bn(G[:]), op=ALU.mult),
                     r=['identf', k("G")], w=['n_X1'])
                pdG, pkG = dbank()
                S.mm([lambda e: e.matmul(pdG[:, 0:512], lhsT=onesF[:], rhs=X1[:, 0:512], start=True, stop=True),
                      lambda e: e.matmul(pdG[:, 512:1024], lhsT=onesF[:], rhs=X1[:, 512:1024], start=True, stop=True)],
                     r=['n_X1', 'n_ones'], w=pkG)
                pbk, pkk = bank()
                pbkb = pbk.bitcast(BF16)
                S.mm([lambda e, h=h: e.transpose(pbkb[:, h * 128:(h + 1) * 128], qk[:, 8 + h, cs], identb[:, :])
                      for h in range(8)], r=qkk[8:16] + ['identb'], w=[pkk])
                S.op('dve', lambda e: e.tensor_tensor(out=h3(B["kdec"][:]), in0=h3(pbkb), in1=bn(kdw[:]), op=ALU.mult),
                     r=[pkk, k("kdw")], w=[bk("kdec")])
                S.op('dve', lambda e: e.tensor_tensor(out=h3(kbg[:]), in0=h3(pbkb), in1=bn(cvec[:]), op=ALU.mult),
                     r=[pkk, k("cvec")], w=[kbg.name])
                pbv, pkv = bank()
                pbvb = pbv.bitcast(BF16)
                S.mm([lambda e, h=h: e.transpose(pbvb[:, h * 128:(h + 1) * 128], vT[:, h, cs], identb[:, :])
                      for h in range(8)], r=vTk + ['identb'], w=[pkv])
                S.op('dve', lambda e: e.tensor_tensor(out=h3(B["vb"][:]), in0=h3(pbvb), in1=bn(bet[:]), op=ALU.mult),
                     r=[pkv, k("bet")], w=[bk("vb")])
                yield
                S.op('dve', lambda e: e.tensor_tensor(out=h3(X1[:]), in0=h3(pdG), in1=bn(G[:]), op=ALU.subtract),
                     r=pkG + [k("G")], w=['n_X1'])
                S.op('act', lambda e: e.activation(out=X4[:], in_=pdG, func=AF.Exp), r=pkG, w=['n_X4'])
                S.op('act', lambda e: e.activation(out=X1[:], in_=X1[:], func=AF.Abs), r=['n_X1'], w=['n_X1'])
                S.op('act', lambda e: e.activation(out=X1[:], in_=X1[:], func=AF.Exp, scale=-1.0), r=['n_X1'], w=['n_X1'])
                S.op('pool', lambda e: e.tensor_tensor(out=h3(B["qg"][:]), in0=qk[:, 0:8, cs], in1=h3(X4[:]), op=ALU.mult),
                     r=['n_X4'] + qkk[0:8], w=[bk("qg")])
                S.op('dve', lambda e: e.tensor_tensor(out=h3(X2[:]), in0=h3(X1[:]), in1=bh(mL), op=ALU.mult),
                     r=['n_X1', 'n_mL'], w=['n_X2'])
                S.op('dve', lambda e: e.tensor_tensor(out=h3(X2[:]), in0=h3(X2[:]), in1=bn(nbet[:]), op=ALU.mult),
                     r=['n_X2', k("nbet")], w=['n_X2'])
                S.op('pool', lambda e: e.tensor_tensor(out=h3(X4[:]), in0=h3(X1[:]), in1=bh(mU), op=ALU.mult),
                     r=['n_X1', 'n_mU'], w=['n_X4'])
                yield
                pdK, pkK = dbank()
                S.mm([lambda e, h=h: e.matmul(pdK[:, h * 128:(h + 1) * 128], lhsT=qk[:, 8 + h, cs], rhs=qk[:, 8 + h, cs],
                                              start=True, stop=True) for h in range(8)], r=qkk[8:16], w=pkK)
                S.op('dve', lambda e: e.tensor_tensor(out=NtS[:], in0=pdK, in1=X2[:], op=ALU.mult), r=pkK + ['n_X2'],
                     w=[('n_NtS', 0), ('n_NtS', 1)])
                pdQ, pkQ = dbank()
                S.mm([lambda e, h=h: e.matmul(pdQ[:, h * 128:(h + 1) * 128], lhsT=qk[:, 8 + h, cs], rhs=qk[:, h, cs],
                                              start=True, stop=True) for h in range(8)], r=qkk, w=pkQ)
                S.op('dve', lambda e: e.tensor_tensor(out=B["attnT"][:], in0=pdQ, in1=X4[:], op=ALU.mult),
                     r=pkQ + ['n_X4'], w=[bk("attnT")])
                yield
                pdM, pkM = dbank()
                S.mm([lambda e, h=h: e.matmul(pdM[:, h * 128:(h + 1) * 128], lhsT=NtS[:, h * 128:(h + 1) * 128],
                                              rhs=identr_r, start=True, stop=True) for h in range(8)],
                     r=[('n_NtS', 0), ('n_NtS', 1), 'identb'], w=pkM)
                S.op('act', lambda e: e.activation(out=MPS[:, :, 0:128], in_=h3(pdM), func=AF.Copy), r=pkM,
                     w=[('n_MPS', 0), ('n_MPS', 1)])
                S.op('pool', lambda e: e.tensor_copy(out=MPS[:, :, 128:256], in_=bh(identf)),
                     r=['identf', ('n_MPS', 0), ('n_MPS', 1)], w=[('n_MPS', 0), ('n_MPS', 1)])
                yield

            def levels(B, kbg):
                bk = lambda n: B[n].name
                for l in range(6):
                    if l == 0:
                        Nc, Mc, Mcf, sN, sM = NtS[:], MPS[:], MPS, 'n_NtS', 'n_MPS'
                    else:
                        Nc, Mc, Mcf, sN, sM = Ntr[(l - 1) % 2], MPr[(l - 1) % 2], MP[(l - 1) % 2], 'n_Nt%d' % ((l - 1) % 2), \
                            'n_MP%d' % ((l - 1) % 2)
                    Nn, Mn, dN, dM = Ntr[l % 2], MPr[l % 2], 'n_Nt%d' % (l % 2), 'n_MP%d' % (l % 2)
                    if l < 5:
                        pdN, pkN = dbank()
                    else:
                        pdP, pkP = dbank()
                    for half in range(2):
                        hs_ = slice(4 * half, 4 * half + 4)
                        kNc, kMc = (sN, half), (sM, half)
                        kNn, kMn = (dN, half), (dM, half)
                        if l < 5:
                            pq, pkq = dbank()
                            S.mm([lambda e, hh=hh, pq=pq, Nc=Nc, Mc=Mc, half=half: e.matmul(
                                pq[:, hh * 256:(hh + 1) * 256], lhsT=Nc[:, (4 * half + hh) * 128:(4 * half + hh + 1) * 128],
                                rhs=Mc[:, 4 * half + hh, :], start=True, stop=True) for hh in range(4)],
                                r=[kNc, kMc], w=pkq)
                            S.mm([lambda e, hh=hh, pdN=pdN, Nc=Nc, Mc=Mc, half=half: e.matmul(
                                pdN[:, (4 * half + hh) * 128:(4 * half + hh + 1) * 128], lhsT=Mc[:, 4 * half + hh, 0:128],
                                rhs=Nc[:, (4 * half + hh) * 128:(4 * half + hh + 1) * 128], start=True, stop=True)
                                for hh in range(4)], r=[kNc, kMc], w=[pkN[half]])
                            pv = pq.rearrange("p (h n) -> p h n", h=4)
                            S.op('dve', lambda e, pv=pv, hs_=hs_, Mn=Mn: e.tensor_copy(out=Mn[:, hs_, 0:128],
                                                                                      in_=pv[:, :, 0:128]),
                                 r=pkq, w=[kMn])
                            S.op('dve', lambda e, pv=pv, hs_=hs_, Mn=Mn, Mcf=Mcf: e.tensor_tensor(
                                out=Mn[:, hs_, 128:256], in0=pv[:, :, 128:256], in1=Mcf[:, hs_, 128:256], op=ALU.add),
                                r=pkq + [kMc, kMn], w=[kMn])
                            S.op('act', lambda e, pdN=pdN, Nn=Nn, half=half: e.activation(
                                out=Nn[:, half * 512:(half + 1) * 512], in_=pdN[:, half * 512:(half + 1) * 512], func=AF.Copy),
                                r=[pkN[half]], w=[kNn])
                        else:
                            S.mm([lambda e, hh=hh, pdP=pdP, Nc=Nc, Mc=Mc, half=half: e.matmul(
                                pdP[:, (4 * half + hh) * 128:(4 * half + hh + 1) * 128],
                                lhsT=Nc[:, (4 * half + hh) * 128:(4 * half + hh + 1) * 128], rhs=Mc[:, 4 * half + hh, 128:256],
                                start=True, stop=True) for hh in range(4)], r=[kNc, kMc], w=[pkP[half]])
                            S.op('dve', lambda e, pdP=pdP, Mcf=Mcf, hs_=hs_, half=half: e.tensor_tensor(
                                out=h3(B["TT"][:])[:, hs_, :], in0=h3(pdP)[:, hs_, :], in1=Mcf[:, hs_, 128:256], op=ALU.add),
                                r=[pkP[half], kMc], w=[(bk("TT"), half)])
                    yield
                pdC, pkC = dbank()
                S.mm([lambda e, h=h: e.matmul(pdC[:, h * 128:(h + 1) * 128], lhsT=kbg[:, h * 128:(h + 1) * 128],
                                              rhs=B["TT"][:, h * 128:(h + 1) * 128], start=True, stop=True)
                      for h in range(8)], r=[kbg.name, (bk("TT"), 0), (bk("TT"), 1)], w=pkC)
                S.op('act', lambda e: e.activation(out=B["nkcT"][:], in_=pdC, func=AF.Copy), r=pkC, w=[bk("nkcT")])
                yield

            def chain(m, B, sm, is_last):
                c0 = 16 + 128 * m
                bk = lambda n: B[n].name
                TT3, vb3, nk3, qg3, at3, kd3 = (h3(B[n][:]) for n in ("TT", "vb", "nkcT", "qg", "attnT", "kdec"))
                for X in range(2):
                    pr = slice(64 * X, 64 * X + 64)
                    cX = slice(c0 + 64 * X, c0 + 64 * X + 64)
                    vnk = [('n_vn', X, 0), ('n_vn', X, 1)]
                    pd1, pk1 = dbank()
                    fns = []
                    for h in range(8):
                        fns.append(lambda e, h=h, pd1=pd1, pr=pr: e.matmul(
                            pd1[pr, h * 128:(h + 1) * 128], lhsT=TT3[pr, h, pr], rhs=vb3[pr, h, :], start=(h % 4 == 0),
                            stop=False, skip_group_check=True))
                    for h in range(8):
                        fns.append(lambda e, h=h, pd1=pd1, pr=pr: e.matmul(
                            pd1[pr, h * 128:(h + 1) * 128], lhsT=nk3[:, h, pr], rhs=Sb[:, h, :], start=False, stop=True,
                            skip_group_check=True))
                    S.mm(fns, r=[(bk("TT"), 0), (bk("TT"), 1), bk("vb"), bk("nkcT"), 'Sb'], w=pk1)
                    S.op('act', lambda e, pd1=pd1, pr=pr: e.activation(out=vn[pr, 0:512], in_=pd1[pr, 0:512], func=AF.Copy),
                         r=[pk1[0]], w=[vnk[0]])
                    S.op('dve', lambda e, pd1=pd1, pr=pr: e.tensor_copy(out=vn[pr, 512:1024], in_=pd1[pr, 512:1024]),
                         r=[pk1[1]], w=[vnk[1]])
                    yield
                    pbo, pko = bank()
                    fns = []
                    for h in range(8):
                        fns.append(lambda e, h=h, pbo=pbo, pr=pr: e.matmul(
                            pbo[:, h * 64:(h + 1) * 64], lhsT=Sb[:, h, :], rhs=qg3[:, h, pr], start=(h == 0), stop=False,
                            skip_group_check=True))
                    for h in range(8):
                        fns.append(lambda e, h=h, pbo=pbo, pr=pr: e.matmul(
                            pbo[:, h * 64:(h + 1) * 64], lhsT=vn[pr, h * 128:(h + 1) * 128], rhs=at3[pr, h, pr],
                            start=False, stop=True, skip_group_check=True))
                    S.mm(fns, r=['Sb', bk("qg"), bk("attnT")] + vnk, w=[pko])
                    S.op('act', lambda e, pbo=pbo, cX=cX: e.activation(
                        out=vT[:, :, cX], in_=pbo.rearrange("p (h n) -> p h n", h=8), func=AF.Copy), r=[pko], w=vTk)
                    pd3, pk3 = dbank()
                    S.mm([lambda e, h=h, pd3=pd3, pr=pr: e.matmul(
                        pd3[:, h * 128:(h + 1) * 128], lhsT=kd3[pr, h, :], rhs=vn[pr, h * 128:(h + 1) * 128], start=True,
                        stop=True) for h in range(8)], r=[bk("kdec")] + vnk, w=pk3)
                    S.op('pool', lambda e, X=X: e.tensor_tensor(out=Sf[:], in0=Sf[:], in1=bn(sm["egl2"][:, 8 * X:8 * X + 8]),
                                                                op=ALU.mult), r=['Sf', sm["egl2"].name], w=['Sf'])
                    S.op('dve', lambda e, pd3=pd3: e.tensor_tensor(out=Sb[:], in0=Sf[:], in1=h3(pd3), op=ALU.add),
                         r=['Sf'] + pk3, w=['Sb'])
                    S.op('dve', lambda e, pd3=pd3: e.tensor_tensor(out=Sf[:], in0=Sf[:], in1=h3(pd3), op=ALU.add),
                         r=['Sf'] + pk3, w=['Sf'])
                    yield
                if is_last:
                    S.dma(s_out[0].rearrange("h k v -> k h v"), Sf[:], r=['Sf'])

            def drain(g):
                for _ in g:
                    pass

            def step(g):
                return g is not None and next(g, 'END') != 'END'

            tcs = lambda m: slice(16 + 128 * m, 144 + 128 * m)
            for m in range(min(3, NPAIR)):
                drain(small(tcs(m), sms[m]))
            drain(setup(tcs(0), Bs[0], sms[0], kbgs[0]))
            drain(levels(Bs[0], kbgs[0]))
            if NPAIR > 1:
                drain(setup(tcs(1), Bs[1], sms[1], kbgs[1]))
            for m in range(NPAIR):
                gc = chain(m, Bs[m % 2], sms[m % 4], m == NPAIR - 1)
                gl = levels(Bs[(m + 1) % 2], kbgs[(m + 1) % 2]) if m + 1 < NPAIR else None
                gs = setup(tcs(m + 2), Bs[m % 2], sms[(m + 2) % 4], kbgs[m % 2]) if m + 2 < NPAIR else None
                gsm = small(tcs(m + 3), sms[(m + 3) % 4]) if m + 3 < NPAIR else None
                while True:
                    a = step(gl)
                    b_ = step(gc)
                    if not b_:
                        break
                while True:
                    a = step(gl)
                    b_ = step(gs)
                    b_ = step(gs) or b_
                    c_ = step(gsm)
                    if not (a or b_ or c_):
                        break
            if os.environ.get("OLD_S", "0") == "1":
                wk.close()
                end_phase()
                return
            S.barrier()
            for t, key in ((mU, 'n_mU'), (mL, 'n_mL')):
                S.op('pool', lambda e, t=t: e.memset(t[:], 1.0), w=[key])
                S.op('pool', lambda e, t=t: e.affine_select(out=t[:], in_=t[:], pattern=[[-4, 32], [0, 4]],
                                                            compare_op=ALU.is_ge, fill=0.0, base=0, channel_multiplier=1),
                     r=[key], w=[key])
                S.op('pool', lambda e, t=t: e.affine_select(out=t[:], in_=t[:], pattern=[[4, 32], [0, 4]],
                                                            compare_op=ALU.is_ge, fill=0.0, base=3, channel_multiplier=-1),
                     r=[key], w=[key])
            S.op('pool', lambda e: e.affine_select(out=mU[:], in_=mU[:], pattern=[[4, 32], [1, 4]], compare_op=ALU.is_ge,
                                                   fill=0.0, base=0, channel_multiplier=-1), r=['n_mU'], w=['n_mU'])
            S.op('pool', lambda e: e.affine_select(out=mL[:], in_=mL[:], pattern=[[-4, 32], [-1, 4]], compare_op=ALU.is_gt,
                                                   fill=0.0, base=0, channel_multiplier=1), r=['n_mL'], w=['n_mL'])
            B = Bs[0]
            drain(small(slice(2000, 2128), sms[0]))
            drain(setup(slice(2000, 2128), B, sms[0], kbgs[0]))
            drain(levels(B, kbgs[0]))
            S.barrier()
            wk.close()
            ps_ = slice(64, 128)
            TT3, vb3, nk3, qg3, at3 = (h3(B[n][:]) for n in ("TT", "vb", "nkcT", "qg", "attnT"))
            nkMs = [sbl("s_nkM%d" % i, [128, 2, 8, 64], BF16) for i in range(2)]
            kdM = sbl("s_kdM", [128, 2, 1024], BF16)
            Sgs = [sbl("s_Sg%d" % i, [128, 2, 8, 128], F32) for i in range(2)]
            Sbgs = [sbl("s_Sbg%d" % i, [128, 2, 8, 128], BF16) for i in range(2)]
            rowm = sbl("s_rowm", [128, 16], F32)
            Rt = sbl("s_R", [128, 16, 8], F32)
            egs = sbl("s_egs", [128, 16, 8], F32)
            g = sms[0]["g"]
            S.op('pool', lambda e: e.memset(rowm[:], 1.0), w=['s_rowm'])
            S.op('pool', lambda e: e.affine_select(out=rowm[:], in_=rowm[:], pattern=[[-4, 16]], compare_op=ALU.is_ge,
                                                   fill=0.0, base=-64, channel_multiplier=1), r=['s_rowm'], w=['s_rowm'])
            S.op('pool', lambda e: e.affine_select(out=rowm[:], in_=rowm[:], pattern=[[4, 16]], compare_op=ALU.is_ge,
                                                   fill=0.0, base=67, channel_multiplier=-1), r=['s_rowm'], w=['s_rowm'])
            S.op('dve', lambda e: e.tensor_tensor(out=Rt[:], in0=rowm[:, :].unsqueeze(2).to_broadcast([128, 16, 8]),
                                                  in1=g[:, :].unsqueeze(1).to_broadcast([128, 16, 8]), op=ALU.mult),
                 r=['s_rowm', sms[0]["g"].name], w=['s_R'])
            pE, pkE = bank()
            S.mm([lambda e: e.matmul(pE[:, 0:128], lhsT=onesF[:], rhs=Rt[:].rearrange("p b h -> p (b h)"), start=True,
                                     stop=True)], r=['s_R', 'n_ones'], w=[pkE])
            S.op('act', lambda e: e.activation(out=egs[:].rearrange("p b h -> p (b h)"), in_=pE[:, 0:128], func=AF.Exp),
                 r=[pkE], w=['s_egs'])
            pdv, pkv_ = dbank()
            pbo, pko = bank()
            S.mm([lambda e, h=h: e.matmul(pdv[ps_, h * 128:(h + 1) * 128], lhsT=TT3[ps_, h, ps_], rhs=vb3[ps_, h, :],
                                          start=(h % 4 == 0), stop=False, skip_group_check=True) for h in range(8)],
                 r=[(B["TT"].name, 0), (B["TT"].name, 1), B["vb"].name], w=pkv_)

            def samp_pass1(gi):
                Sg, Sbg, nkM = Sgs[gi % 2], Sbgs[gi % 2], nkMs[gi % 2]
                S.dma(Sg[:], s0[2 * gi:2 * gi + 2].rearrange("b h k v -> k b h v"), w=[Sg.name])
                S.op('act', lambda e: e.activation(out=Sbg[:], in_=Sg[:], func=AF.Copy), r=[Sg.name], w=[Sbg.name])
                S.op('pool', lambda e: e.memset(nkM[:], 0.0), w=[nkM.name])
                for bi in range(2):
                    b = 2 * gi + bi
                    S.op('pool', lambda e, b=b, bi=bi: e.tensor_copy(out=nkM[:, bi, :, 4 * b:4 * b + 4],
                                                                     in_=nk3[:, :, 64 + 4 * b:68 + 4 * b]),
                         r=[B["nkcT"].name, nkM.name], w=[nkM.name])
                fns = []
                for bi in range(2):
                    b = 2 * gi + bi
                    for h in range(8):
                        fns.append(lambda e, b=b, bi=bi, h=h: e.matmul(
                            pdv[ps_, h * 128:(h + 1) * 128], lhsT=nkM[:, bi, h, :], rhs=Sbg[:, bi, h, :], start=False,
                            stop=(b == 15), skip_group_check=True))
                        fns.append(lambda e, b=b, bi=bi, h=h: e.matmul(
                            pbo[:, h * 64 + 4 * b:h * 64 + 4 * b + 4], lhsT=Sbg[:, bi, h, :],
                            rhs=qg3[:, h, 64 + 4 * b:68 + 4 * b], start=(b == 0 and h == 0), stop=False,
                            skip_group_check=True))
                S.mm(fns, r=[nkM.name, Sbg.name, B["qg"].name], w=pkv_ + [pko])

            for gi in range(8):
                samp_pass1(gi)
            S.op('act', lambda e: e.activation(out=vn[ps_, 0:512], in_=pdv[ps_, 0:512], func=AF.Copy), r=[pkv_[0]],
                 w=['s_vn0'])
            S.op('dve', lambda e: e.tensor_copy(out=vn[ps_, 512:1024], in_=pdv[ps_, 512:1024]), r=[pkv_[1]], w=['s_vn1'])
            S.mm([lambda e, h=h: e.matmul(pbo[:, h * 64:(h + 1) * 64], lhsT=vn[ps_, h * 128:(h + 1) * 128],
                                          rhs=at3[ps_, h, ps_], start=False, stop=True,
                                          skip_group_check=True) for h in range(8)],
                 r=['s_vn0', 's_vn1', B["attnT"].name], w=[pko])
            S.op('act', lambda e: e.activation(out=vT[:, :, NP:NT], in_=pbo.rearrange("p (h n) -> p h n", h=8),
                                               func=AF.Copy), r=[pko], w=vTk)

            def samp_pass2(gi):
                Sg = Sgs[gi % 2]
                S.dma(Sg[:], s0[2 * gi:2 * gi + 2].rearrange("b h k v -> k b h v"), w=[Sg.name])
                pss = []
                for bi in range(2):
                    b = 2 * gi + bi
                    S.op('dve', lambda e, b=b, bi=bi: e.tensor_scalar(out=kdM[ps_, bi, :], in0=B["kdec"][ps_, :],
                                                                      scalar1=rowm[ps_, b:b + 1], scalar2=None,
                                                                      op0=ALU.mult),
                         r=[B["kdec"].name, 's_rowm'], w=[('s_kdM', bi)])
                    pS, pkS = dbank()
                    S.mm([lambda e, bi=bi, h=h, pS=pS: e.matmul(pS[:, h * 128:(h + 1) * 128],
                                                                lhsT=kdM[ps_, bi, h * 128:(h + 1) * 128],
                                                                rhs=vn[ps_, h * 128:(h + 1) * 128], start=True, stop=True)
                          for h in range(8)], r=[('s_kdM', bi), 's_vn0', 's_vn1'], w=pkS)
                    pss.append((pS, pkS))
                S.op('pool', lambda e: e.tensor_tensor(
                    out=Sg[:], in0=Sg[:],
                    in1=egs[:, 2 * gi:2 * gi + 2, :].unsqueeze(3).to_broadcast([128, 2, 8, 128]), op=ALU.mult),
                    r=[Sg.name, 's_egs'], w=[Sg.name])
                for bi, (pS, pkS) in enumerate(pss):
                    S.op('dve', lambda e, bi=bi, pS=pS: e.tensor_tensor(out=Sg[:, bi, :, :], in0=Sg[:, bi, :, :],
                                                                        in1=pS.rearrange("p (h n) -> p h n", h=8),
                                                                        op=ALU.add), r=[Sg.name] + pkS, w=[Sg.name])
                S.dma(s_out[1 + 2 * gi:3 + 2 * gi].rearrange("b h k v -> k b h v"), Sg[:], r=[Sg.name])

            for gi in range(8):
                samp_pass2(gi)
            end_phase()

        NEWPATH = os.environ.get("OLD_B", "0") != "1"
        if not NEWPATH:
            old_path(list(range(len(_chunks()))), "a")
        else:
            old_path([0], "a")
            new_pairs()
            if os.environ.get("OLD_S", "0") == "1":
                old_path(list(range(33, 49)), "c")
        if debug:
            dbg['oT'] = dout("dbg_oT", [128, 8, NT], BF16)
            S.dma(dbg['oT'], vT[:], r=vTk)
        end_phase()

    if upto >= 3:
        ws2 = WStream([(w_in, OFF_ZB + h * 128) for h in range(8)], 3, 4, 2)
        b2 = [{n: sbl("b2_%s%d" % (n, i), [128, 512], dt) for n, dt in (("sz", BF16), ("sq", BF16), ("ln", F32),
                                                                        ("tmp", F32))} for i in range(4)]
        b2n = [0]

        def b2_norm(h, ti, t0, t1):
            n = t1 - t0
            B = b2[b2n[0] % 4]
            b2n[0] += 1
            sq, ln = B["sq"], B["ln"]
            S.op('pool', lambda e: e.tensor_tensor(out=sq[:, 0:n], in0=vT[:, h, t0:t1], in1=vT[:, h, t0:t1], op=ALU.mult),
                 r=[('vT', h)], w=[sq.name])
            pss, pks = bank()
            S.mm([lambda e: e.matmul(pss[:, 0:n], lhsT=onesb[:], rhs=sq[:, 0:n], start=True, stop=True)],
                 r=[sq.name, 'onesb'], w=[pks])
            S.op('act', lambda e: e.activation(out=ln[:, 0:n], in_=pss[:, 0:n], func=AF.Ln, scale=1.0 / 128, bias=EPS),
                 r=[pks], w=[ln.name])
            S.op('act', lambda e: e.activation(out=ln[:, 0:n], in_=ln[:, 0:n], func=AF.Exp, scale=-0.5), r=[ln.name],
                 w=[ln.name])
            S.op('dve', lambda e: e.scalar_tensor_tensor(out=vT[:, h, t0:t1], in0=vT[:, h, t0:t1], scalar=gnorm[:, 0:1],
                                                         in1=ln[:, 0:n], op0=ALU.mult, op1=ALU.mult),
                 r=[('vT', h), ln.name, 'gnorm_s'], w=[('vT', h)])

        def b2_gate(h, wt, wk, ti, t0, t1):
            n = t1 - t0
            sz = b2[b2n[0] % 4]["sz"]
            b2n[0] += 1
            pz, pkz = proj(wt, wk, xnT, 'xnT', t0, t1)
            S.op('act', lambda e: e.activation(out=sz[:, 0:n], in_=pz[:, 0:n], func=AF.Silu), r=[pkz], w=[sz.name])
            S.op('dve', lambda e: e.tensor_tensor(out=vT[:, h, t0:t1], in0=vT[:, h, t0:t1], in1=sz[:, 0:n], op=ALU.mult),
                 r=[('vT', h), sz.name], w=[('vT', h)])

        ws2.get(0)
        for h in range(8):
            for ti, (t0, t1) in enumerate(TBS):
                b2_norm(h, ti, t0, t1)
        for h in range(8):
            wt, wk = ws2.get(h)
            for ti, (t0, t1) in enumerate(TBS):
                b2_gate(h, wt, wk, ti, t0, t1)
        if debug:
            dbg['yb'] = dout("dbg_yb", [128, 8, NT], BF16)
            S.dma(dbg['yb'], vT[:], r=[('vT', h) for h in range(8)])
        end_phase()

    if upto >= 4:
        reqs = []
        for cc in range(8):
            reqs += [(w_in, OFF_H + cc * 128), (w_in, OFF_C + cc * 128), (w_in, OFF_B + cc * 128), (w_in, OFF_Z + cc * 128)]
        ws3 = WStream(reqs, 4, 8, 4)
        pend = []
        ubuf = [sbl("ubuf%d" % i, [128, 2 + NP], F32) for i in range(2)]
        usb = [sbl("us%d" % i, [128, NSQ, 6], F32) for i in range(2)]
        a2 = [{n: sbl("a2_%s%d" % (n, i), [128, 512], F32) for n in ("hs", "cv", "sz", "tmp")} for i in range(2)]
        for i in range(2):
            S.op('pool', lambda e, i=i: e.memset(ubuf[i][:, 0:2], 0.0), w=[ubuf[i].name])

        def do_a2(cc):
            ub, us = ubuf[cc % 2], usb[cc % 2]
            wh, kh = ws3.get(4 * cc)
            wc, kcw = ws3.get(4 * cc + 1)
            wb_, kb = ws3.get(4 * cc + 2)
            wz, kz = ws3.get(4 * cc + 3)
            for f in pend:
                f()
            pend.clear()
            S.dma(us[:, :, 0:2], sca[:, cc, :, :], w=[us.name])
            cw = lambda j: cwa[:, cc * 3 + j:cc * 3 + j + 1]

            def do_tb(ti, t0, t1):
                n = t1 - t0
                A = a2[ti % 2]
                hs, cv, sz, tmp = A["hs"], A["cv"], A["sz"], A["tmp"]
                p_h, k_h = proj(wh, kh, xnT, 'xnT', t0, t1)
                p_c, k_c = proj(wc, kcw, xnT, 'xnT', t0, t1)
                p_b, k_b = proj(wb_, kb, xnT, 'xnT', t0, t1)
                p_z, k_z = proj(wz, kz, xnT, 'xnT', t0, t1)
                S.op('act', lambda e: e.activation(out=hs[:, 0:n], in_=p_h[:, 0:n], func=AF.Copy), r=[k_h], w=[hs.name])
                npr = min(t1, NP) - t0
                S.op('dve', lambda e: e.tensor_tensor(out=ub[:, 2 + t0:2 + t0 + npr], in0=p_c[:, 0:npr], in1=hs[:, 0:npr],
                                                      op=ALU.mult), r=[k_c, hs.name], w=[ub.name])
                S.op('dve', lambda e: e.tensor_scalar(out=cv[:, 0:npr], in0=ub[:, t0:t0 + npr], scalar1=cw(0), scalar2=None,
                                                      op0=ALU.mult), r=[ub.name, 'cwa_s'], w=[cv.name])
                for j in (1, 2):
                    S.op('dve', lambda e, j=j: e.scalar_tensor_tensor(
                        out=cv[:, 0:npr], in0=ub[:, t0 + j:t0 + j + npr], scalar=cw(j), in1=cv[:, 0:npr], op0=ALU.mult,
                        op1=ALU.add), r=[ub.name, cv.name], w=[cv.name])
                if t1 > NP:
                    v4 = lambda ap: ap.rearrange("p (b t) -> p b t", t=4)
                    S.op('dve', lambda e: e.tensor_tensor(out=us[:, :, 2:6], in0=v4(p_c[:, npr:n]), in1=v4(hs[:, npr:n]),
                                                          op=ALU.mult), r=[k_c, hs.name], w=[us.name])
                    S.op('dve', lambda e: e.tensor_scalar(out=v4(cv[:, npr:n]), in0=us[:, :, 0:4], scalar1=cw(0),
                                                          scalar2=None, op0=ALU.mult), r=[us.name, 'cwa_s'], w=[cv.name])
                    for j in (1, 2):
                        S.op('dve', lambda e, j=j: e.scalar_tensor_tensor(
                            out=v4(cv[:, npr:n]), in0=us[:, :, j:j + 4], scalar=cw(j), in1=v4(cv[:, npr:n]), op0=ALU.mult,
                            op1=ALU.add), r=[us.name, cv.name], w=[cv.name])
                S.op('act', lambda e: e.activation(out=sz[:, 0:n], in_=p_z[:, 0:n], func=AF.Silu), r=[k_z], w=[sz.name])
                S.op('dve', lambda e: e.tensor_tensor(out=tmp[:, 0:n], in0=p_b[:, 0:n], in1=cv[:, 0:n], op=ALU.mult),
                     r=[k_b, cv.name], w=[tmp.name])
                S.op('pool', lambda e: e.tensor_tensor(out=qk[:, cc, t0:t1], in0=tmp[:, 0:n], in1=sz[:, 0:n], op=ALU.mult),
                     r=[tmp.name, sz.name], w=[('qk', cc)])

            for ti, (t0, t1) in enumerate(TBS):
                do_tb(ti, t0, t1)
            pend.append(lambda: S.dma(oca[:, cc, 0, :], ub[:, NP:NP + 2], r=[ub.name]))
            pend.append(lambda: S.dma(oca[:, cc, 1:1 + NSQ, :], us[:, :, 4:6], r=[us.name]))

        for cc in range(8):
            do_a2(cc)
        for f in pend:
            f()
        pend.clear()
        if debug:
            dbg['ya'] = dout("dbg_ya", [128, 8, NT], BF16)
            S.dma(dbg['ya'], qk[:, 0:8, :], r=[('qk', i) for i in range(8)])
        end_phase()

    if upto >= 5:
        reqs = []
        for oc in range(8):
            reqs += [(w_a_out, oc * 128), (w_b_out, oc * 128), (w_in, OFF_GA + oc * 128), (w_in, OFF_GB + oc * 128)]
        ws4 = WStream(reqs, 4, 8, 4)
        c2 = [{n: sbl("c_%s%d" % (n, i), [128, 512], F32) for n in ("sa", "sb", "m1", "m2")} for i in range(2)]

        def do_c(oc):
            wa, ka = ws4.get(4 * oc)
            wb_, kb = ws4.get(4 * oc + 1)
            wga, kga = ws4.get(4 * oc + 2)
            wgb, kgb = ws4.get(4 * oc + 3)

            def do_tb(ti, t0, t1):
                n = t1 - t0
                C = c2[ti % 2]
                sa, sb_, m1, m2 = C["sa"], C["sb"], C["m1"], C["m2"]
                p_a, k_a = proj(wa, ka, qk, None, t0, t1, extra_r=[('qk', i) for i in range(8)])
                p_ga, k_ga = proj(wga, kga, xnT, 'xnT', t0, t1)
                p_b, k_b = proj(wb_, kb, vT, None, t0, t1, extra_r=[('vT', i) for i in range(8)])
                p_gb, k_gb = proj(wgb, kgb, xnT, 'xnT', t0, t1)
                S.op('act', lambda e: e.activation(out=sa[:, 0:n], in_=p_ga[:, 0:n], func=AF.Sigmoid,
                                                   bias=bgate[:, oc:oc + 1]), r=[k_ga, 'bgate_s'], w=[sa.name])
                S.op('act', lambda e: e.activation(out=sb_[:, 0:n], in_=p_gb[:, 0:n], func=AF.Sigmoid,
                                                   bias=bgate[:, 8 + oc:9 + oc]), r=[k_gb, 'bgate_s'], w=[sb_.name])
                S.op('dve', lambda e: e.tensor_tensor(out=m1[:, 0:n], in0=p_a[:, 0:n], in1=sa[:, 0:n], op=ALU.mult),
                     r=[k_a, sa.name], w=[m1.name])
                S.op('dve', lambda e: e.tensor_tensor(out=m2[:, 0:n], in0=p_b[:, 0:n], in1=sb_[:, 0:n], op=ALU.mult),
                     r=[k_b, sb_.name], w=[m2.name])
                S.op('pool', lambda e: e.tensor_tensor(out=qk[:, 8 + oc, t0:t1], in0=m1[:, 0:n], in1=m2[:, 0:n], op=ALU.add),
                     r=[m1.name, m2.name], w=[('qk', 8 + oc)])

            for ti, (t0, t1) in enumerate(TBS):
                do_tb(ti, t0, t1)

        for oc in range(8):
            do_c(oc)
        if debug:
            dbg['mg'] = dout("dbg_mg", [128, 8, NT], BF16)
            S.dma(dbg['mg'], qk[:, 8:16, :], r=[('qk', 8 + i) for i in range(8)])
        end_phase()

    if upto >= 6:
        wst[:] = [sbl("wstf%d" % i, [128, 8, 128], F32) for i in range(3)]
        xt2 = [sbl("fx%d" % i, [128, D], F32) for i in range(2)]
        yt2 = [sbl("fy%d" % i, [128, D], F32) for i in range(2)]
        junk2 = sbl("fjunk", [128, D], BF16)
        npost = sbl("npost_s", [128, D], F32)
        S.dma(npost[:], npost_d, w=['npost_s'])
        fsq = [sbl("fsq%d" % i, [128, 1], F32) for i in range(2)]
        wo = xnT
        for j in range(8):
            st = wst[j % 3]
            S.dma(st[:], w_o[:, j * 128:(j + 1) * 128].rearrange("(kc p) f -> p kc f", p=128), w=[st.name])
            S.op('pool', lambda e, st=st, j=j: e.tensor_copy(out=wo[:, :, j * 128:(j + 1) * 128], in_=st[:]),
                 r=[st.name], w=['xnT'])
        ftiles = [(16 + 128 * i, 128) for i in range(16)] + [(NP, NS)]
        mgk = [('qk', 8 + i) for i in range(8)]

        def do_final(i, r0, n):
            x_, y_, q_ = xt2[i % 2], yt2[i % 2], fsq[i % 2]
            S.dma(x_[0:n, :], x_all[r0:r0 + n, :], w=[x_.name])
            pd, pks = dbank()
            fns = []
            for half in range(2):
                for kc in range(8):
                    fns.append(lambda e, half=half, kc=kc: e.matmul(
                        pd[0:n, half * 512:(half + 1) * 512], lhsT=qk[:, 8 + kc, r0:r0 + n],
                        rhs=wo[:, kc, half * 512:(half + 1) * 512], start=(kc == 0), stop=(kc == 7)))
            S.mm(fns, r=mgk + ['xnT'], w=pks)
            S.op('act', lambda e: e.activation(out=junk2[0:n, :], in_=pd[0:n, :], func=AF.Square, accum_out=q_[0:n, :]),
                 r=pks, w=['fjunk', q_.name])
            S.op('act', lambda e: e.activation(out=q_[0:n, :], in_=q_[0:n, :], func=AF.Sqrt, scale=1.0 / D, bias=EPS),
                 r=[q_.name], w=[q_.name])
            S.op('dve', lambda e: e.reciprocal(out=q_[0:n, :], in_=q_[0:n, :]), r=[q_.name], w=[q_.name])
            S.op('dve', lambda e: e.scalar_tensor_tensor(out=y_[0:n, :], in0=pd[0:n, :], scalar=q_[0:n, 0:1],
                                                         in1=npost[0:n, :], op0=ALU.mult, op1=ALU.mult),
                 r=pks + [q_.name, 'npost_s'], w=[y_.name])
            S.op('pool', lambda e: e.tensor_tensor(out=y_[0:n, :], in0=y_[0:n, :], in1=x_[0:n, :], op=ALU.add),
                 r=[y_.name, x_.name], w=[y_.name])
            S.dma(y_out[r0 - 16:r0 - 16 + n, :], y_[0:n, :], r=[y_.name])

        for i, (r0, n) in enumerate(ftiles):
            do_final(i, r0, n)

    S.finish()
    ph[0].close()
    block = es.enter_context(nc.Block())
    S.emit(block)
    es.close()
    return nc, dbg


def prep_inputs(inp, c):
    f = lambda a: np.ascontiguousarray(a, dtype=np.float32)
    xs = inp["x_sample"][16 * c:16 * c + 16].reshape(NS, D)
    x_all = np.concatenate([inp["meta"], inp["x_prompt"][c], xs], axis=0)
    scq = inp["state_conv_qkv"][0, 16 * c:16 * c + 16]
    scq = scq.reshape(16, 3, 24, 128).transpose(3, 2, 0, 1)
    sca = inp["state_conv_a"][0, 16 * c:16 * c + 16].reshape(16, 2, 8, 128).transpose(3, 2, 0, 1)
    m = {
        "x_all": f(x_all), "w_in": f(inp["w_in"][0]), "w_a_out": f(inp["w_a_out"][0]), "w_b_out": f(inp["w_b_out"][0]),
        "w_o": f(inp["w_o"][0]), "scq": f(scq), "sca": f(sca), "s0": f(inp["state_delta"][0, 16 * c:16 * c + 16]),
        "cwq": f(inp["conv_qkv_w"][0].reshape(4, 24, 128).transpose(2, 1, 0).reshape(128, 96)),
        "cwa": f(inp["conv_a_w"][0].reshape(3, 8, 128).transpose(2, 1, 0).reshape(128, 24)),
        "bgate": f(inp["b_gate"][0].reshape(16, 128).T), "gnorm": f(inp["gnorm_w"][0].reshape(128, 1)),
        "npre_bc": f(np.broadcast_to(inp["norm_pre"][0][None, :], (128, D))),
        "npost_bc": f(np.broadcast_to(inp["norm_post"][0][None, :], (128, D))),
        "alog_bc": f(np.broadcast_to(inp["a_log"][0][None, :], (128, 8))),
        "dtb_bc": f(np.broadcast_to(inp["dt_bias"][0][None, :], (128, 8))),
    }
    return m


def run(inputs, upto=99, debug=False, cores=NCORES):
    inp = {k: np.asarray(v) for k, v in inputs.items()}
    nc, dbg = build(upto=upto, debug=debug)
    in_maps = [prep_inputs(inp, c) for c in range(cores)]
    res = run_bass_kernel_spmd(nc, in_maps, core_ids=list(range(cores)))
    return res.results


def kernel(**inputs):
    rs = run(inputs)
    B = NCORES
    y_prompt = np.stack([r["y_out"][0:2048] for r in rs]).astype(np.float32)
    y_sample = np.concatenate([r["y_out"][2048:].reshape(NSQ, 4, D) for r in rs]).astype(np.float32)
    cq = [np.asarray(r["ocq"]).transpose(2, 3, 1, 0).reshape(1 + NSQ, 3, 3072) for r in rs]
    ca = [np.asarray(r["oca"]).transpose(2, 3, 1, 0).reshape(1 + NSQ, 2, 1024) for r in rs]
    so = [np.asarray(r["s_out"]) for r in rs]
    conv_a_prompt = np.stack([a[0] for a in ca])[None].astype(np.float32)
    conv_qkv_prompt = np.stack([a[0] for a in cq])[None].astype(np.float32)
    delta_prompt = np.stack([a[0] for a in so])[None].astype(np.float32)
    conv_a_sample = np.concatenate([a[1:] for a in ca])[None].astype(np.float32)
    conv_qkv_sample = np.concatenate([a[1:] for a in cq])[None].astype(np.float32)
    delta_sample = np.concatenate([a[1:] for a in so])[None].astype(np.float32)
    return (y_prompt, y_sample, conv_a_prompt, conv_qkv_prompt, delta_prompt,
            conv_a_sample, conv_qkv_sample, delta_sample)
```

```python
import os
import numpy as np
from contextlib import ExitStack
import concourse.bass as bass
import concourse.mybir as mybir
from concourse.bass_utils import run_bass_kernel_spmd

F32 = mybir.dt.float32
BF16 = mybir.dt.bfloat16
AF = mybir.ActivationFunctionType
ALU = mybir.AluOpType

NCORES = 8
D = 1024
NP = 2064
NSQ = 16
NS = 64
NT = NP + NS
NIN = 10256
EPS = 1e-6
TBS = [(0, 512), (512, 1024), (1024, 1536), (1536, 2048), (2048, 2128)]
OFF_H, OFF_B, OFF_C, OFF_Z, OFF_QKV, OFF_ZB, OFF_BA, OFF_GA, OFF_GB = 0, 1024, 2048, 3072, 4096, 7168, 8192, 8208, 9232
ENG = ['pe', 'act', 'dve', 'pool', 'sp']
NDS = 24


class Sched:
    def __init__(self, nc, es):
        self.nc = nc
        self.q = {e: [] for e in ENG}
        self.sem = {e: es.enter_context(nc.semaphore("s_" + e)) for e in ENG}
        self.cnt = {e: 0 for e in ENG}
        self.seen = {e: {} for e in ENG}
        self.lastw = {}
        self.rd = {}
        self.dsem = [es.enter_context(nc.semaphore("d%d" % i)) for i in range(NDS)]
        self.dcnt = [0] * NDS
        self.dnext = 0
        self.nops = 0

    def _wait(self, e, toks):
        for tok in toks:
            kind, who, val = tok
            if kind == 'e' and who == e and e == 'pe':
                continue
            key = (kind, who)
            if self.seen[e].get(key, 0) >= val:
                continue
            self.seen[e][key] = val
            s = self.sem[who] if kind == 'e' else self.dsem[who]
            self.q[e].append(lambda eng, s=s, v=val: eng.wait_ge(s, v))

    @staticmethod
    def _excl(r, w):
        return list(w) + [k for k in r if isinstance(k, tuple) and k[0] == 'ps' and k not in w]

    def _deps(self, r, w):
        toks = []
        for k in r:
            toks += self.lastw.get(k, [])
        for k in w:
            toks += self.lastw.get(k, [])
            toks += self.rd.get(k, [])
        return toks

    def _record(self, tok, r, w):
        for k in r:
            lst = self.rd.setdefault(k, [])
            lst[:] = [t for t in lst if not (t[0] == tok[0] and t[1] == tok[1])]
            lst.append(tok)
        for k in w:
            self.lastw[k] = [tok]
            self.rd[k] = []

    def op(self, e, fn, r=(), w=()):
        w = self._excl(r, w)
        self._wait(e, self._deps(r, w))
        self.cnt[e] += 1
        s = self.sem[e]
        self.q[e].append(lambda eng, fn=fn, s=s: fn(eng).then_inc(s, 1))
        self._record(('e', e, self.cnt[e]), r, w)
        self.nops += 1

    def mm(self, fns, r=(), w=()):
        w = self._excl(r, w)
        self._wait('pe', self._deps(r, w))
        for fn in fns[:-1]:
            self.q['pe'].append(lambda eng, fn=fn: fn(eng))
        self.cnt['pe'] += 1
        s = self.sem['pe']
        self.q['pe'].append(lambda eng, fn=fns[-1], s=s: fn(eng).then_inc(s, 1))
        self._record(('e', 'pe', self.cnt['pe']), r, w)
        self.nops += len(fns)

    def dma(self, out, in_, r=(), w=(), e='sp', **kw):
        slot = self.dnext
        self.dnext = (self.dnext + 1) % NDS
        toks = self._deps(r, w)
        if self.dcnt[slot] > 0:
            toks.append(('d', slot, self.dcnt[slot]))
        self._wait(e, toks)
        self.dcnt[slot] += 16
        s = self.dsem[slot]
        self.q[e].append(lambda eng, o=out, i=in_, s=s, kw=kw: eng.dma_start(out=o, in_=i, **kw).then_inc(s, 16))
        self._record(('d', slot, self.dcnt[slot]), r, w)

    def barrier(self):
        toks = [('e', e, self.cnt[e]) for e in ENG if self.cnt[e] > 0]
        toks += [('d', i, self.dcnt[i]) for i in range(NDS) if self.dcnt[i] > 0]
        for e in ENG:
            self._wait(e, [t for t in toks if not (t[0] == 'e' and t[1] == e)])

    def finish(self):
        toks = [('d', i, self.dcnt[i]) for i in range(NDS) if self.dcnt[i] > 0]
        toks += [('e', e, self.cnt[e]) for e in ENG if self.cnt[e] > 0]
        self._wait('sp', toks)

    def emit(self, block):
        q = self.q

        @block.sync
        def _(eng):
            for f in q['sp']:
                f(eng)

        @block.scalar
        def _(eng):
            for f in q['act']:
                f(eng)

        @block.vector
        def _(eng):
            for f in q['dve']:
                f(eng)

        @block.gpsimd
        def _(eng):
            for f in q['pool']:
                f(eng)

        @block.tensor
        def _(eng):
            for f in q['pe']:
                f(eng)


def _chunks():
    ch = [(0, 16, 0)]
    for n in range(32):
        ch.append((16 + 64 * n, 64, 0))
    for b in range(NSQ):
        ch.append((NP + 4 * b, 4, 1 + b))
    return ch


def build(upto=99, debug=False):
    nc = bass.Bass("TRN2", target_bir_lowering=False)
    es = ExitStack()

    def din(name, shape, dt=F32):
        return nc.dram_tensor(name, list(shape), dt, kind="ExternalInput").ap()

    def dout(name, shape, dt=F32):
        return nc.dram_tensor(name, list(shape), dt, kind="ExternalOutput").ap()

    x_all = din("x_all", [NT, D])
    w_in = din("w_in", [D, NIN])
    w_a_out = din("w_a_out", [D, D])
    w_b_out = din("w_b_out", [D, D])
    w_o = din("w_o", [D, D])
    scq = din("scq", [128, 24, NSQ, 3])
    sca = din("sca", [128, 8, NSQ, 2])
    s0 = din("s0", [NSQ, 8, 128, 128])
    cwq_d = din("cwq", [128, 24 * 4])
    cwa_d = din("cwa", [128, 8 * 3])
    bgate_d = din("bgate", [128, 16])
    gnorm_d = din("gnorm", [128, 1])
    npre_d = din("npre_bc", [128, D])
    npost_d = din("npost_bc", [128, D])
    alog_d = din("alog_bc", [128, 8])
    dtb_d = din("dtb_bc", [128, 8])

    y_out = dout("y_out", [NT - 16, D])
    ocq = dout("ocq", [128, 24, 1 + NSQ, 3])
    oca = dout("oca", [128, 8, 1 + NSQ, 2])
    s_out = dout("s_out", [1 + NSQ, 8, 128, 128])
    dbg = {}

    def sb(name, shape, dt=F32):
        return es.enter_context(nc.sbuf_tensor(name, list(shape), dt))

    S = Sched(nc, es)
    ph = [ExitStack()]

    def sbl(name, shape, dt=F32):
        return ph[0].enter_context(nc.sbuf_tensor(name, list(shape), dt))

    def end_phase():
        S.barrier()
        ph[0].close()
        ph[0] = ExitStack()

    xnT = sb("xnT", [128, 8, NT], BF16)
    qk = sb("qk", [128, 16, NT], BF16)
    vT = sb("vT", [128, 8, NT], BF16)
    identf = sb("identf", [128, 128], F32)
    identb = sb("identb", [128, 128], BF16)
    onesb = sb("onesb", [128, 128], BF16)
    cwq = sb("cwq_s", [128, 24 * 4], F32)
    cwa = sb("cwa_s", [128, 8 * 3], F32)
    bgate = sb("bgate_s", [128, 16], F32)
    gnorm = sb("gnorm_s", [128, 1], F32)
    negA = sb("negA", [128, 8], F32)
    dtb = sb("dtb_s", [128, 8], F32)
    wba = sb("wba", [128, 8, 16], BF16)
    wbaf = sbl("wbaf", [128, 8, 16], F32)
    ps = [es.enter_context(nc.psum_tensor("ps%d" % i, [128, 1024], F32)) for i in range(4)]
    psn = [0]

    def bank():
        i = psn[0] % 8
        psn[0] += 1
        return ps[i // 2][:, (i % 2) * 512:(i % 2) * 512 + 512], ('ps', i)

    def dbank():
        if psn[0] % 2:
            psn[0] += 1
        i = psn[0] % 8
        psn[0] += 2
        return ps[i // 2][:, :], [('ps', i), ('ps', i + 1)]

    for t, k in ((cwq, cwq_d), (cwa, cwa_d), (bgate, bgate_d), (gnorm, gnorm_d),
                 (negA, alog_d), (dtb, dtb_d)):
        S.dma(t[:], k, w=[t.name])
    S.op('act', lambda e: e.activation(out=negA[:], in_=negA[:], func=AF.Exp), r=['negA'], w=['negA'])
    S.op('dve', lambda e: e.tensor_scalar(out=negA[:], in0=negA[:], scalar1=-1.0, scalar2=None, op0=ALU.mult),
         r=['negA'], w=['negA'])
    S.op('pool', lambda e: e.memset(identf[:], 0.0), w=['identf'])
    S.op('pool', lambda e: e.affine_select(out=identf[:], in_=identf[:], pattern=[[-1, 128]], compare_op=ALU.not_equal,
                                           fill=1.0, base=0, channel_multiplier=1), r=['identf'], w=['identf'])
    S.op('pool', lambda e: e.tensor_copy(out=identb[:], in_=identf[:]), r=['identf'], w=['identb'])
    S.op('pool', lambda e: e.memset(onesb[:], 1.0), w=['onesb'])
    S.dma(wbaf[:], w_in[:, OFF_BA:OFF_BA + 16].rearrange("(kc p) f -> p kc f", p=128), w=['wbaf'])
    S.op('pool', lambda e: e.tensor_copy(out=wba[:], in_=wbaf[:]), r=['wbaf'], w=['wba'])

    wst = []
    wgen = [0]

    class WStream:
        def __init__(self, reqs, nst, nbf, depth):
            wgen[0] += 1
            g = wgen[0]
            self.st = [sbl("wst%d_%d" % (i, g), [128, 8, 128], F32) for i in range(nst)]
            self.bf = [sbl("wbf%d_%d" % (i, g), [128, 8, 128], BF16) for i in range(nbf)]
            wst[:] = self.st
            self.reqs, self.depth, self.issued = reqs, depth, 0

        def _issue(self, i):
            src, f0 = self.reqs[i]
            st, bf = self.st[i % len(self.st)], self.bf[i % len(self.bf)]
            S.dma(st[:], src[:, f0:f0 + 128].rearrange("(kc p) f -> p kc f", p=128), w=[st.name])
            S.op('pool', lambda e: e.tensor_copy(out=bf[:], in_=st[:]), r=[st.name], w=[bf.name])

        def get(self, i):
            while self.issued < min(len(self.reqs), i + self.depth + 1):
                self._issue(self.issued)
                self.issued += 1
            bf = self.bf[i % len(self.bf)]
            return bf, bf.name

    def proj(wt, wkey, src, skey, t0, t1, extra_r=()):
        pb, pk = bank()
        n = t1 - t0
        fns = []
        for kc in range(8):
            fns.append(lambda e, kc=kc: e.matmul(pb[:, 0:n], lhsT=wt[:, kc, :], rhs=src[:, kc, t0:t1],
                                                 start=(kc == 0), stop=(kc == 7)))
        S.mm(fns, r=[wkey] + ([skey] if skey else []) + list(extra_r), w=[pk])
        return pb, pk

    npre = sbl("npre_s", [128, D], F32)
    S.dma(npre[:], npre_d, w=['npre_s'])
    xt = [sbl("xt%d" % i, [128, D], F32) for i in range(3)]
    xnf = [sbl("xnf%d" % i, [128, D], F32) for i in range(3)]
    junk = sbl("junk", [128, D], BF16)
    ssq = [sbl("ssq%d" % i, [128, 1], F32) for i in range(3)]
    tiles = [(i * 128, min(128, NT - i * 128)) for i in range((NT + 127) // 128)]
    for ti, (r0, n) in enumerate(tiles):
        x_, xn_, sq_ = xt[ti % 3], xnf[ti % 3], ssq[ti % 3]
        S.dma(x_[0:n, :], x_all[r0:r0 + n, :], w=[x_.name])
        S.op('act', lambda e, x_=x_, sq_=sq_, n=n: e.activation(out=junk[0:n, :], in_=x_[0:n, :], func=AF.Square,
                                                                 accum_out=sq_[0:n, :]),
             r=[x_.name], w=['junk', sq_.name])
        S.op('act', lambda e, sq_=sq_, n=n: e.activation(out=sq_[0:n, :], in_=sq_[0:n, :], func=AF.Sqrt,
                                                          scale=1.0 / D, bias=EPS), r=[sq_.name], w=[sq_.name])
        S.op('dve', lambda e, sq_=sq_, n=n: e.reciprocal(out=sq_[0:n, :], in_=sq_[0:n, :]), r=[sq_.name], w=[sq_.name])
        S.op('dve', lambda e, x_=x_, xn_=xn_, sq_=sq_, n=n: e.scalar_tensor_tensor(
            out=xn_[0:n, :], in0=x_[0:n, :], scalar=sq_[0:n, 0:1], in1=npre[0:n, :], op0=ALU.mult, op1=ALU.mult),
            r=[x_.name, sq_.name, 'npre_s'], w=[xn_.name])
        pd, pks = dbank()
        fns = []
        for kc in range(8):
            fns.append(lambda e, kc=kc, xn_=xn_, n=n, pd=pd: e.transpose(pd[:, kc * 128:kc * 128 + n],
                                                                          xn_[0:n, kc * 128:(kc + 1) * 128],
                                                                          identf[0:n, 0:n]))
        S.mm(fns, r=[xn_.name, 'identf'], w=pks)
        src = pd.rearrange("p (kc t) -> p kc t", kc=8)[:, :, 0:n]
        S.op('act' if ti % 2 else 'dve',
             (lambda e, src=src, r0=r0, n=n: e.activation(out=xnT[:, :, r0:r0 + n], in_=src, func=AF.Copy)) if ti % 2 else
             (lambda e, src=src, r0=r0, n=n: e.tensor_copy(out=xnT[:, :, r0:r0 + n], in_=src)),
             r=pks, w=['xnT'])

    if debug:
        dbg['xnT'] = dout("dbg_xnT", [128, 8, NT], BF16)
        S.dma(dbg['xnT'], xnT[:], r=['xnT'])
    end_phase()

    if upto >= 1:
        ws = WStream([(w_in, OFF_QKV + qc * 128) for qc in range(24)], 3, 4, 2)
        pend = []
        xq = [sbl("xq%d" % i, [128, 3 + NP], F32) for i in range(2)]
        xs = [sbl("xs%d" % i, [128, NSQ, 7], F32) for i in range(2)]
        accs_ = [sbl("acc%d" % i, [128, NT], F32) for i in range(2)]
        sqbs_ = [sbl("sqb%d" % i, [128, NT], BF16) for i in range(2)]
        lnbs_ = [sbl("lnb%d" % i, [128, 512], F32) for i in range(2)]
        for i in range(2):
            S.op('pool', lambda e, i=i: e.memset(xq[i][:, 0:3], 0.0), w=[(xq[i].name, 0)])
        ctx = {}

        def a1_ctx(qc):
            c = dict(which=qc // 8, h=qc % 8, xq=xq[qc % 2], xs=xs[qc % 2], acc=accs_[qc % 2], sqb=sqbs_[qc % 2])
            c["wt"], c["wk"] = ws.get(qc)
            for f in pend:
                f()
            pend.clear()
            S.dma(c["xs"][:, :, 0:3], scq[:, qc, :, :], w=[c["xs"].name])
            ctx[qc] = c
            return c

        def stageA(qc, ti):
            c = ctx[qc] if ti > 0 else a1_ctx(qc)
            t0, t1 = TBS[ti]
            xq_, xs_ = c["xq"], c["xs"]
            pb, pk = proj(c["wt"], c["wk"], xnT, 'xnT', t0, t1)
            npr = min(t1, NP) - t0
            S.op('act', lambda e: e.activation(out=xq_[:, 3 + t0:3 + t0 + npr], in_=pb[:, 0:npr], func=AF.Copy),
                 r=[pk], w=[(xq_.name, ti)])
            if t1 > NP:
                S.op('act', lambda e: e.activation(
                    out=xs_[:, :, 3:7], in_=pb[:, npr:npr + NS].rearrange("p (b t) -> p b t", t=4), func=AF.Copy),
                    r=[pk], w=[xs_.name])
                pend.append(lambda: S.dma(ocq[:, qc, 0, :], xq_[:, NP:NP + 3], r=[(xq_.name, 4)]))
                pend.append(lambda: S.dma(ocq[:, qc, 1:1 + NSQ, :], xs_[:, :, 4:7], r=[xs_.name]))

        def stageB(qc, ti):
            c = ctx[qc]
            t0, t1 = TBS[ti]
            xq_, xs_, acc, sqb, which, h = c["xq"], c["xs"], c["acc"], c["sqb"], c["which"], c["h"]
            npr = min(t1, NP) - t0
            cw = lambda j: cwq[:, qc * 4 + j:qc * 4 + j + 1]
            ak = (acc.name, ti)
            rk = [(xq_.name, ti)] + ([(xq_.name, ti - 1)] if ti > 0 else [])
            S.op('pool', lambda e: e.tensor_scalar(out=acc[:, t0:t0 + npr], in0=xq_[:, t0:t0 + npr], scalar1=cw(0),
                                                   scalar2=0.0, op0=ALU.mult, op1=ALU.add), r=rk + ['cwq_s'], w=[ak])
            for j in (1, 2, 3):
                S.op('dve', lambda e, j=j: e.scalar_tensor_tensor(
                    out=acc[:, t0:t0 + npr], in0=xq_[:, t0 + j:t0 + j + npr], scalar=cw(j), in1=acc[:, t0:t0 + npr],
                    op0=ALU.mult, op1=ALU.add), r=rk + [ak], w=[ak])
            if t1 > NP:
                accs = acc[:, NP:NT].rearrange("p (b t) -> p b t", t=4)
                S.op('pool', lambda e: e.tensor_scalar(out=accs, in0=xs_[:, :, 0:4], scalar1=cw(0), scalar2=0.0,
                                                       op0=ALU.mult, op1=ALU.add), r=[xs_.name, ak], w=[ak])
                for j in (1, 2, 3):
                    S.op('dve', lambda e, j=j: e.scalar_tensor_tensor(
                        out=accs, in0=xs_[:, :, j:j + 4], scalar=cw(j), in1=accs, op0=ALU.mult, op1=ALU.add),
                        r=[xs_.name, ak], w=[ak])
            if which == 2:
                S.op('act', lambda e: e.activation(out=vT[:, h, t0:t1], in_=acc[:, t0:t1], func=AF.Silu), r=[ak],
                     w=[('vT', h)])
                return
            S.op('act', lambda e: e.activation(out=acc[:, t0:t1], in_=acc[:, t0:t1], func=AF.Silu), r=[ak], w=[ak])
            S.op('act', lambda e: e.activation(out=sqb[:, t0:t1], in_=acc[:, t0:t1], func=AF.Square), r=[ak],
                 w=[(sqb.name, ti)])

        def stageC(qc):
            c = ctx[qc]
            acc, sqb, which = c["acc"], c["sqb"], c["which"]
            lnscale = float(np.log(128.0 ** -0.5)) if which == 0 else 0.0

            def one(ti, t0, t1):
                n = t1 - t0
                lnb = lnbs_[ti % 2]
                pb2, pk2 = bank()
                S.mm([lambda e: e.matmul(pb2[:, 0:n], lhsT=onesb[:], rhs=sqb[:, t0:t1], start=True, stop=True)],
                     r=[(sqb.name, ti), 'onesb'], w=[pk2])
                S.op('act', lambda e: e.activation(out=lnb[:, 0:n], in_=pb2[:, 0:n], func=AF.Ln, bias=EPS), r=[pk2],
                     w=[lnb.name])
                S.op('act', lambda e: e.activation(out=lnb[:, 0:n], in_=lnb[:, 0:n], func=AF.Exp, scale=-0.5, bias=lnscale),
                     r=[lnb.name], w=[lnb.name])
                S.op('dve', lambda e: e.tensor_tensor(out=qk[:, qc, t0:t1], in0=acc[:, t0:t1], in1=lnb[:, 0:n], op=ALU.mult),
                     r=[(acc.name, ti), lnb.name], w=[('qk', qc)])

            for ti, (t0, t1) in enumerate(TBS):
                one(ti, t0, t1)

        items = [(qc, ti) for qc in range(24) for ti in range(5)]
        SK = 2
        for i in range(len(items) + SK):
            if i < len(items):
                stageA(*items[i])
            j = i - SK
            if j >= 0:
                stageB(*items[j])
                if items[j][1] == 4 and items[j][0] < 16:
                    stageC(items[j][0])
        for f in pend:
            f()
        pend.clear()
        if debug:
            dbg['qk'] = dout("dbg_qk", [128, 16, NT], BF16)
            S.dma(dbg['qk'], qk[:], r=[('qk', i) for i in range(16)])
            dbg['vT'] = dout("dbg_vT", [128, 8, NT], BF16)
            S.dma(dbg['vT'], vT[:], r=[('vT', i) for i in range(8)])
        end_phase()

    if upto >= 2:
        Sf = sb("b_Sf", [128, 8, 128], F32)
        Sb = sb("b_Sb", [128, 8, 128], BF16)
        vTk = [('vT', h) for h in range(8)]
        qkk = [('qk', i) for i in range(16)]

        def old_path(cidx, tag):
            onesf = sbl("onesf" + tag, [64, 128], F32)
            triU = sbl("triU" + tag, [64, 64], F32)
            S.op('pool', lambda e: e.memset(onesf[:], 1.0), w=['onesf'])
            S.op('pool', lambda e: e.memset(triU[:], 1.0), w=['triU'])
            S.op('pool', lambda e: e.affine_select(out=triU[:], in_=triU[:], pattern=[[1, 64]], compare_op=ALU.is_ge,
                                                   fill=0.0, base=0, channel_multiplier=-1), r=['triU'], w=['triU'])
            maskU8 = sbl("maskU8" + tag, [64, 8, 64], F32)
            maskL8 = sbl("maskL8" + tag, [64, 8, 64], F32)
            ident8 = sbl("ident8" + tag, [64, 8, 64], F32)
            S.op('pool', lambda e: e.memset(maskU8[:], 1.0), w=['maskU8'])
            S.op('pool', lambda e: e.affine_select(out=maskU8[:], in_=maskU8[:], pattern=[[0, 8], [1, 64]],
                                                   compare_op=ALU.is_ge, fill=0.0, base=0, channel_multiplier=-1),
                 r=['maskU8'], w=['maskU8'])
            S.op('pool', lambda e: e.memset(maskL8[:], 1.0), w=['maskL8'])
            S.op('pool', lambda e: e.affine_select(out=maskL8[:], in_=maskL8[:], pattern=[[0, 8], [-1, 64]],
                                                   compare_op=ALU.is_gt, fill=0.0, base=0, channel_multiplier=1),
                 r=['maskL8'], w=['maskL8'])
            S.op('pool', lambda e: e.memset(ident8[:], 0.0), w=['ident8'])
            S.op('pool', lambda e: e.affine_select(out=ident8[:], in_=ident8[:], pattern=[[0, 8], [-1, 64]],
                                                   compare_op=ALU.not_equal, fill=1.0, base=0, channel_multiplier=1),
                 r=['ident8'], w=['ident8'])
            sm = {n: sbl("b%s_" % tag + n, [64, 8], F32) for n in ("bet", "nbet", "g", "G", "kdw", "eG", "cvec")}
            egl = sbl("b%s_" % tag + "egl", [128, 8], F32)
            fl = {n: sbl("b%s_" % tag + n, [64, 512], F32) for n in ("Dm", "nad", "t1", "t2", "N0", "N1", "M0", "M1", "Pm")}
            attnT = sbl("b%s_" % tag + "attnT", [64, 512], BF16)
            TT = sbl("b%s_" % tag + "TT", [64, 512], BF16)
            EG = sbl("b%s_" % tag + "EG", [128, 512], F32)
            qg = sbl("b%s_" % tag + "qg", [128, 512], BF16)
            kdec = sbl("b%s_" % tag + "kdec", [64, 1024], BF16)
            vb = sbl("b%s_" % tag + "vb", [64, 1024], BF16)
            rhs2 = sbl("b%s_" % tag + "rhs2", [64, 1024], BF16)
            vn = sbl("b%s_" % tag + "vn", [64, 1024], BF16)

            def v3(t, L, P=None):
                P = L if P is None else P
                return t[0:P, 0:8 * L].rearrange("p (h n) -> p h n", h=8)

            def w3(t, L):
                return t[0:L, :].rearrange("p (h n) -> p h n", h=8)

            def bc(t, L, n, P=None):
                P = L if P is None else P
                return t[0:P, :].unsqueeze(2).to_broadcast([P, 8, n])

            vTk = [('vT', h) for h in range(8)]
            qkk = [('qk', i) for i in range(16)]
            chunks = _chunks()
            STOP = int(os.environ.get("DBG_STOP", "99"))

            def do_chunk(ci, c0, L, seq):
                first = (ci == 0) or (chunks[ci - 1][2] != seq)
                last = (ci == len(chunks) - 1) or (chunks[ci + 1][2] != seq)
                nlev = {64: 5, 16: 3, 4: 1}[L]
                cs = slice(c0, c0 + L)
                if first:
                    if seq == 0:
                        S.op('pool', lambda e: e.memset(Sf[:], 0.0), w=['Sf'])
                        S.op('pool', lambda e: e.memset(Sb[:], 0.0), w=['Sb'])
                    else:
                        S.dma(Sf[:], s0[seq - 1].rearrange("h k v -> k h v"), w=['Sf'])
                        S.op('act', lambda e: e.activation(out=Sb[:], in_=Sf[:], func=AF.Copy), r=['Sf'], w=['Sb'])
                bet, nbet, g, G, kdw, eG, cvec = (sm[n] for n in ("bet", "nbet", "g", "G", "kdw", "eG", "cvec"))
                pb, pk = bank()
                S.mm([lambda e, kc=kc, pb=pb: e.matmul(pb[0:L, 0:16], lhsT=xnT[:, kc, cs], rhs=wba[:, kc, :],
                                                       start=(kc == 0), stop=(kc == 7)) for kc in range(8)],
                     r=['xnT', 'wba'], w=[pk])
                S.op('act', lambda e, pb=pb: e.activation(out=bet[0:L, :], in_=pb[0:L, 0:8], func=AF.Sigmoid), r=[pk], w=['bet'])
                S.op('dve', lambda e, pb=pb: e.tensor_tensor(out=g[0:L, :], in0=pb[0:L, 8:16], in1=dtb[0:L, :], op=ALU.add),
                     r=[pk, 'dtb_s'], w=['g'])
                S.op('act', lambda e: e.activation(out=g[0:L, :], in_=g[0:L, :], func=AF.Exp), r=['g'], w=['g'])
                S.op('act', lambda e: e.activation(out=g[0:L, :], in_=g[0:L, :], func=AF.Ln, bias=1.0), r=['g'], w=['g'])
                S.op('dve', lambda e: e.tensor_tensor(out=g[0:L, :], in0=g[0:L, :], in1=negA[0:L, :], op=ALU.mult),
                     r=['g', 'negA'], w=['g'])
                S.op('dve', lambda e: e.tensor_scalar(out=nbet[0:L, :], in0=bet[0:L, :], scalar1=-1.0, scalar2=None,
                                                      op0=ALU.mult), r=['bet'], w=['nbet'])
                if STOP < 3:
                    return
                pb2, pk2 = bank()
                S.mm([lambda e, pb2=pb2: e.matmul(pb2[0:L, 0:8], lhsT=triU[0:L, 0:L], rhs=g[0:L, :], start=True, stop=True),
                      lambda e, pb2=pb2: e.matmul(pb2[:, 8:16], lhsT=onesf[0:L, :], rhs=g[0:L, :], start=True, stop=True)],
                     r=['g', 'triU', 'onesf'], w=[pk2])
                S.op('dve', lambda e, pb2=pb2: e.tensor_copy(out=G[0:L, :], in_=pb2[0:L, 0:8]), r=[pk2], w=['G'])
                S.op('act', lambda e, pb2=pb2: e.activation(out=egl[:], in_=pb2[:, 8:16], func=AF.Exp), r=[pk2], w=['egl'])
                S.op('dve', lambda e, pb2=pb2: e.tensor_tensor(out=kdw[0:L, :], in0=pb2[0:L, 8:16], in1=G[0:L, :],
                                                               op=ALU.subtract), r=[pk2, 'G'], w=['kdw'])
                S.op('act', lambda e: e.activation(out=kdw[0:L, :], in_=kdw[0:L, :], func=AF.Exp), r=['kdw'], w=['kdw'])
                S.op('act', lambda e: e.activation(out=eG[0:L, :], in_=G[0:L, :], func=AF.Exp), r=['G'], w=['eG'])
                S.op('dve', lambda e: e.tensor_tensor(out=cvec[0:L, :], in0=nbet[0:L, :], in1=eG[0:L, :], op=ALU.mult),
                     r=['nbet', 'eG'], w=['cvec'])
                if STOP < 4:
                    return
                Dm, nad, t1, t2, Pm = fl["Dm"], fl["nad"], fl["t1"], fl["t2"], fl["Pm"]
                S.op('dve', lambda e: e.tensor_tensor(out=v3(Dm, L), in0=ident8[0:L, :, 0:L], in1=bc(G, L, L), op=ALU.mult),
                     r=['G', 'ident8'], w=['Dm'])
                pbG, pkG = bank()
                S.mm([lambda e, pbG=pbG: e.matmul(pbG[:, 0:8 * L], lhsT=onesf[0:L, :], rhs=Dm[0:L, 0:8 * L], start=True,
                                                  stop=True)], r=['Dm', 'onesf'], w=[pkG])
                if STOP < 5:
                    return
                S.op('dve', lambda e, pbG=pbG: e.tensor_tensor(out=v3(nad, L), in0=v3(pbG, L), in1=bc(G, L, L),
                                                               op=ALU.subtract), r=[pkG, 'G'], w=['nad'])
                S.op('act', lambda e: e.activation(out=nad[0:L, 0:8 * L], in_=nad[0:L, 0:8 * L], func=AF.Abs), r=['nad'],
                     w=['nad'])
                S.op('act', lambda e: e.activation(out=nad[0:L, 0:8 * L], in_=nad[0:L, 0:8 * L], func=AF.Exp, scale=-1.0),
                     r=['nad'], w=['nad'])
                S.op('act', lambda e, pbG=pbG: e.activation(out=EG[:, 0:8 * L], in_=pbG[:, 0:8 * L], func=AF.Exp), r=[pkG],
                     w=['EG'])
                S.op('pool', lambda e: e.tensor_tensor(out=v3(qg, L, 128), in0=qk[:, 0:8, cs], in1=v3(EG, L, 128),
                                                       op=ALU.mult), r=['EG'] + qkk[0:8], w=['qg'])
                if STOP < 6:
                    return
                pbK, pkK = bank()
                S.mm([lambda e, h=h, pbK=pbK: e.matmul(pbK[0:L, h * L:(h + 1) * L], lhsT=qk[:, 8 + h, cs], rhs=qk[:, 8 + h, cs],
                                                       start=True, stop=True) for h in range(8)], r=qkk[8:16], w=[pkK])
                pbQ, pkQ = bank()
                S.mm([lambda e, h=h, pbQ=pbQ: e.matmul(pbQ[0:L, h * L:(h + 1) * L], lhsT=qk[:, 8 + h, cs], rhs=qk[:, h, cs],
                                                       start=True, stop=True) for h in range(8)], r=qkk, w=[pkQ])
                if STOP < 7:
                    return
                N, M = [fl["N0"], fl["N1"]], [fl["M0"], fl["M1"]]
                S.op('dve', lambda e, pbK=pbK: e.tensor_tensor(out=v3(t1, L), in0=v3(pbK, L), in1=v3(nad, L), op=ALU.mult),
                     r=[pkK, 'nad'], w=['t1'])
                S.op('pool', lambda e: e.tensor_tensor(out=v3(t1, L), in0=v3(t1, L), in1=maskL8[0:L, :, 0:L], op=ALU.mult),
                     r=['t1', 'maskL8'], w=['t1'])
                S.op('dve', lambda e: e.tensor_tensor(out=v3(N[0], L), in0=v3(t1, L), in1=bc(nbet, L, L), op=ALU.mult),
                     r=['t1', 'nbet'], w=['N0'])
                S.op('dve', lambda e, pbQ=pbQ: e.tensor_tensor(out=v3(t2, L), in0=v3(pbQ, L), in1=v3(nad, L), op=ALU.mult),
                     r=[pkQ, 'nad'], w=['t2'])
                S.op('pool', lambda e: e.tensor_tensor(out=v3(attnT, L), in0=v3(t2, L), in1=maskU8[0:L, :, 0:L], op=ALU.mult),
                     r=['t2', 'maskU8'], w=['attnT'])
                if STOP < 8:
                    return
                pbM, pkM = bank()
                S.mm([lambda e, h=h, pbM=pbM: e.matmul(pbM[0:L, h * L:(h + 1) * L], lhsT=N[0][0:L, h * L:(h + 1) * L],
                                                       rhs=identf[0:L, 0:L], start=True, stop=True) for h in range(8)],
                     r=['N0', 'identf'], w=[pkM])
                S.op('act', lambda e, pbM=pbM: e.activation(out=M[0][0:L, 0:8 * L], in_=pbM[0:L, 0:8 * L], func=AF.Copy),
                     r=[pkM], w=['M0'])
                S.op('dve', lambda e, pbM=pbM: e.tensor_tensor(out=v3(Pm, L), in0=v3(pbM, L), in1=ident8[0:L, :, 0:L],
                                                               op=ALU.add), r=[pkM, 'ident8'], w=['Pm'])
                if STOP < 9:
                    return
                cur = 0
                for lev in range(nlev):
                    nx = 1 - cur
                    Nc, Mc, Nn, Mn = N[cur], M[cur], N[nx], M[nx]
                    kNc, kMc, kNn, kMn = 'N%d' % cur, 'M%d' % cur, 'N%d' % nx, 'M%d' % nx
                    lastlev = lev == nlev - 1
                    pbN2, pkN2 = bank()
                    S.mm([lambda e, h=h, pbN2=pbN2, Nc=Nc, Mc=Mc: e.matmul(
                        pbN2[0:L, h * L:(h + 1) * L], lhsT=Mc[0:L, h * L:(h + 1) * L], rhs=Nc[0:L, h * L:(h + 1) * L],
                        start=True, stop=True) for h in range(8)], r=[kNc, kMc], w=[pkN2])
                    S.op('act', lambda e, pbN2=pbN2, Nn=Nn: e.activation(out=Nn[0:L, 0:8 * L], in_=pbN2[0:L, 0:8 * L],
                                                                         func=AF.Copy), r=[pkN2], w=[kNn])
                    if not lastlev:
                        pbM2, pkM2 = bank()
                        S.mm([lambda e, h=h, pbM2=pbM2, Nc=Nc, Mc=Mc: e.matmul(
                            pbM2[0:L, h * L:(h + 1) * L], lhsT=Nc[0:L, h * L:(h + 1) * L], rhs=Mc[0:L, h * L:(h + 1) * L],
                            start=True, stop=True) for h in range(8)], r=[kNc, kMc], w=[pkM2])
                        S.op('dve', lambda e, pbM2=pbM2, Mn=Mn: e.tensor_copy(out=Mn[0:L, 0:8 * L], in_=pbM2[0:L, 0:8 * L]),
                             r=[pkM2], w=[kMn])
                    pbP, pkP = bank()
                    S.mm([lambda e, h=h, pbP=pbP, Nn=Nn: e.matmul(
                        pbP[0:L, h * L:(h + 1) * L], lhsT=Nn[0:L, h * L:(h + 1) * L], rhs=Pm[0:L, h * L:(h + 1) * L],
                        start=True, stop=True) for h in range(8)], r=[kNn, 'Pm'], w=[pkP])
                    S.op('dve', lambda e, pbP=pbP: e.tensor_tensor(out=Pm[0:L, 0:8 * L], in0=Pm[0:L, 0:8 * L],
                                                                   in1=pbP[0:L, 0:8 * L], op=ALU.add), r=[pkP, 'Pm'], w=['Pm'])
                    cur = nx
                S.op('act', lambda e: e.activation(out=TT[0:L, 0:8 * L], in_=Pm[0:L, 0:8 * L], func=AF.Copy), r=['Pm'],
                     w=['TT'])
                if STOP < 10:
                    return
                pbk, pkk = bank()
                pbkb = pbk.bitcast(BF16)
                S.mm([lambda e, h=h, pbkb=pbkb: e.transpose(pbkb[0:L, h * 128:(h + 1) * 128], qk[:, 8 + h, cs], identb[:, :])
                      for h in range(8)], r=qkk[8:16] + ['identb'], w=[pkk])
                S.op('dve', lambda e, pbkb=pbkb: e.tensor_tensor(out=w3(kdec, L), in0=w3(pbkb, L), in1=bc(kdw, L, 128),
                                                                 op=ALU.mult), r=[pkk, 'kdw'], w=['kdec'])
                pbv, pkv = bank()
                pbvb = pbv.bitcast(BF16)
                S.mm([lambda e, h=h, pbvb=pbvb: e.transpose(pbvb[0:L, h * 128:(h + 1) * 128], vT[:, h, cs], identb[:, :])
                      for h in range(8)], r=vTk + ['identb'], w=[pkv])
                S.op('dve', lambda e, pbvb=pbvb: e.tensor_tensor(out=w3(vb, L), in0=w3(pbvb, L), in1=bc(bet, L, 128),
                                                                 op=ALU.mult), r=[pkv, 'bet'], w=['vb'])
                if STOP < 11:
                    return
                pd1, pk1 = dbank()
                S.mm([lambda e, h=h, pd1=pd1: e.matmul(pd1[0:L, h * 128:(h + 1) * 128], lhsT=qk[:, 8 + h, cs], rhs=Sb[:, h, :],
                                                       start=True, stop=True) for h in range(8)], r=qkk[8:16] + ['Sb'], w=pk1)
                S.op('dve', lambda e, pd1=pd1: e.tensor_tensor(out=w3(rhs2, L), in0=w3(pd1, L), in1=bc(cvec, L, 128),
                                                               op=ALU.mult), r=pk1 + ['cvec'], w=['rhs2'])
                S.op('pool', lambda e: e.tensor_tensor(out=w3(rhs2, L), in0=w3(rhs2, L), in1=w3(vb, L), op=ALU.add),
                     r=['rhs2', 'vb'], w=['rhs2'])
                pd2, pk2_ = dbank()
                S.mm([lambda e, h=h, pd2=pd2: e.matmul(pd2[0:L, h * 128:(h + 1) * 128], lhsT=TT[0:L, h * L:(h + 1) * L],
                                                       rhs=rhs2[0:L, h * 128:(h + 1) * 128], start=True, stop=True)
                      for h in range(8)], r=['TT', 'rhs2'], w=pk2_)
                S.op('act', lambda e, pd2=pd2: e.activation(out=vn[0:L, :], in_=pd2[0:L, :], func=AF.Copy), r=pk2_, w=['vn'])
                pbo, pko = bank()
                fns = []
                for h in range(8):
                    fns.append(lambda e, h=h, pbo=pbo: e.matmul(pbo[:, h * L:(h + 1) * L], lhsT=Sb[:, h, :],
                                                                rhs=qg[:, h * L:(h + 1) * L], start=True, stop=False))
                    fns.append(lambda e, h=h, pbo=pbo: e.matmul(pbo[:, h * L:(h + 1) * L], lhsT=vn[0:L, h * 128:(h + 1) * 128],
                                                                rhs=attnT[0:L, h * L:(h + 1) * L], start=False, stop=True))
                S.mm(fns, r=['Sb', 'qg', 'vn', 'attnT'], w=[pko])
                S.op('act', lambda e, pbo=pbo: e.activation(out=vT[:, :, cs], in_=v3(pbo, L, 128), func=AF.Copy), r=[pko],
                     w=vTk)
                pd3, pk3 = dbank()
                S.mm([lambda e, h=h, pd3=pd3: e.matmul(pd3[:, h * 128:(h + 1) * 128], lhsT=kdec[0:L, h * 128:(h + 1) * 128],
                                                       rhs=vn[0:L, h * 128:(h + 1) * 128], start=True, stop=True)
                      for h in range(8)], r=['kdec', 'vn'], w=pk3)
                S.op('dve', lambda e: e.tensor_tensor(out=Sf[:], in0=Sf[:], in1=bc(egl, 128, 128, 128), op=ALU.mult),
                     r=['Sf', 'egl'], w=['Sf'])
                S.op('dve', lambda e, pd3=pd3: e.tensor_tensor(out=Sf[:], in0=Sf[:],
                                                               in1=pd3.rearrange("p (h n) -> p h n", h=8), op=ALU.add),
                     r=['Sf'] + pk3, w=['Sf'])
                S.op('act', lambda e: e.activation(out=Sb[:], in_=Sf[:], func=AF.Copy), r=['Sf'], w=['Sb'])
                if last:
                    S.dma(s_out[seq].rearrange("h k v -> k h v"), Sf[:], r=['Sf'])
                if debug and ci == int(os.environ.get("DBG_CHUNK", "0")):
                    for n, t in list(sm.items()) + list(fl.items()) + [("egl", egl), ("attnT", attnT), ("TT", TT), ("EG", EG),
                                                                       ("qg", qg), ("kdec", kdec), ("vb", vb), ("rhs2", rhs2),
                                                                       ("vn", vn), ("Sf", Sf), ("Sb", Sb)]:
                        dbg[n] = dout("dbgc_" + n, list(t.shape), t.dtype)
                        S.dma(dbg[n], t[:], r=[n])
            for ci in cidx:
                do_chunk(ci, *chunks[ci])
            end_phase()

        def new_pairs():
            F32R = mybir.dt.float32r
            NPAIR = int(os.environ.get("DBG_NPAIR", "16"))
            mU = sbl("n_mU", [128, 128], F32)
            mL = sbl("n_mL", [128, 128], F32)
            onesF = sbl("n_ones", [128, 128], F32)
            S.op('pool', lambda e: e.memset(onesF[:], 1.0), w=['n_ones'])
            for t, key in ((mU, 'n_mU'), (mL, 'n_mL')):
                S.op('pool', lambda e, t=t: e.memset(t[:], 1.0), w=[key])
                S.op('pool', lambda e, t=t: e.affine_select(out=t[:], in_=t[:], pattern=[[-64, 2], [0, 64]],
                                                            compare_op=ALU.is_ge, fill=0.0, base=0, channel_multiplier=1),
                     r=[key], w=[key])
                S.op('pool', lambda e, t=t: e.affine_select(out=t[:], in_=t[:], pattern=[[64, 2], [0, 64]],
                                                            compare_op=ALU.is_ge, fill=0.0, base=63, channel_multiplier=-1),
                     r=[key], w=[key])
            S.op('pool', lambda e: e.affine_select(out=mU[:], in_=mU[:], pattern=[[64, 2], [1, 64]], compare_op=ALU.is_ge,
                                                   fill=0.0, base=0, channel_multiplier=-1), r=['n_mU'], w=['n_mU'])
            S.op('pool', lambda e: e.affine_select(out=mL[:], in_=mL[:], pattern=[[-64, 2], [-1, 64]], compare_op=ALU.is_gt,
                                                   fill=0.0, base=0, channel_multiplier=1), r=['n_mL'], w=['n_mL'])
            kbgs = [sbl("n_kbg%d" % i, [128, 1024], BF16) for i in range(2)]
            vn = sbl("n_vn", [128, 1024], BF16)
            sms = []
            for i in range(4):
                d = {n: sbl("n_%s%d" % (n, i), [128, 8], F32) for n in ("bet", "nbet", "g", "G", "kdw", "cvec")}
                d["egl2"] = sbl("n_egl2%d" % i, [128, 16], F32)
                sms.append(d)
            Bs = []
            for i in range(2):
                d = {n: sbl("n_%s%d" % (n, i), [128, 1024], BF16) for n in ("TT", "attnT", "vb", "kdec", "nkcT", "qg")}
                Bs.append(d)
            wk = ExitStack()
            wkt = lambda name, shape, dt=F32: wk.enter_context(nc.sbuf_tensor(name, list(shape), dt))
            X1 = wkt("n_X1", [128, 1024])
            X2 = wkt("n_X2", [128, 1024])
            X4 = wkt("n_X4", [128, 1024])
            Nt = [wkt("n_Nt%d" % i, [128, 1024], BF16) for i in range(2)]
            MP = [wkt("n_MP%d" % i, [128, 8, 256], BF16) for i in range(2)]
            NtS = wkt("n_NtS", [128, 1024], BF16)
            MPS = wkt("n_MPS", [128, 8, 256], BF16)
            h3 = lambda ap: ap.rearrange("p (h n) -> p h n", h=8)
            bh = lambda t: t[:, :].unsqueeze(1).to_broadcast([128, 8, 128])
            bn = lambda ap: ap.unsqueeze(2).to_broadcast([128, 8, 128])
            Ntr = [t[:] for t in Nt]
            MPr = [t[:] for t in MP]
            identr_r = identb[:, :]

            def small(cs, sm):
                bet, nbet, g, G, kdw, cvec, egl2 = (sm[n] for n in ("bet", "nbet", "g", "G", "kdw", "cvec", "egl2"))
                k = lambda n: sm[n].name
                pb, pk = bank()
                S.mm([lambda e, kc=kc: e.matmul(pb[:, 0:16], lhsT=xnT[:, kc, cs], rhs=wba[:, kc, :], start=(kc == 0),
                                                stop=(kc == 7)) for kc in range(8)], r=['xnT', 'wba'], w=[pk])
                S.op('act', lambda e: e.activation(out=bet[:], in_=pb[:, 0:8], func=AF.Exp, scale=-1.0), r=[pk], w=[k("bet")])
                S.op('dve', lambda e: e.tensor_tensor(out=g[:], in0=pb[:, 8:16], in1=dtb[:], op=ALU.add), r=[pk, 'dtb_s'],
                     w=[k("g")])
                S.op('dve', lambda e: e.tensor_scalar(out=bet[:], in0=bet[:], scalar1=1.0, scalar2=None, op0=ALU.add),
                     r=[k("bet")], w=[k("bet")])
                S.op('dve', lambda e: e.reciprocal(out=bet[:], in_=bet[:]), r=[k("bet")], w=[k("bet")])
                S.op('dve', lambda e: e.tensor_scalar(out=nbet[:], in0=bet[:], scalar1=-1.0, scalar2=None, op0=ALU.mult),
                     r=[k("bet")], w=[k("nbet")])
                S.op('act', lambda e: e.activation(out=g[:], in_=g[:], func=AF.Exp), r=[k("g")], w=[k("g")])
                S.op('act', lambda e: e.activation(out=g[:], in_=g[:], func=AF.Ln, bias=1.0), r=[k("g")], w=[k("g")])
                S.op('dve', lambda e: e.tensor_tensor(out=g[:], in0=g[:], in1=negA[:], op=ALU.mult), r=[k("g"), 'negA'],
                     w=[k("g")])
                yield
                pbs, pks = bank()
                S.mm([lambda e: e.matmul(pbs[:, 0:8], lhsT=mU[:], rhs=g[:], start=True, stop=True),
                      lambda e: e.matmul(pbs[:, 8:16], lhsT=mU[:], rhs=g[:], start=True, stop=False),
                      lambda e: e.matmul(pbs[:, 8:16], lhsT=mL[:], rhs=g[:], start=False, stop=True),
                      lambda e: e.matmul(pbs[:, 16:24], lhsT=onesF[0:64, :], rhs=g[0:64, :], start=True, stop=True),
                      lambda e: e.matmul(pbs[:, 24:32], lhsT=onesF[:, :], rhs=g[:, :], start=True, stop=True)],
                     r=[k("g"), 'n_mU', 'n_mL', 'n_ones'], w=[pks])
                S.op('dve', lambda e: e.tensor_copy(out=G[:], in_=pbs[:, 0:8]), r=[pks], w=[k("G")])
                S.op('dve', lambda e: e.tensor_tensor(out=kdw[:], in0=pbs[:, 8:16], in1=G[:], op=ALU.subtract),
                     r=[pks, k("G")], w=[k("kdw")])
                S.op('dve', lambda e: e.tensor_copy(out=egl2[:], in_=pbs[:, 16:32]), r=[pks], w=[k("egl2")])
                S.op('dve', lambda e: e.tensor_tensor(out=egl2[:, 8:16], in0=egl2[:, 8:16], in1=egl2[:, 0:8],
                                                      op=ALU.subtract), r=[k("egl2")], w=[k("egl2")])
                S.op('act', lambda e: e.activation(out=egl2[:], in_=egl2[:], func=AF.Exp), r=[k("egl2")], w=[k("egl2")])
                S.op('act', lambda e: e.activation(out=kdw[:], in_=kdw[:], func=AF.Exp), r=[k("kdw")], w=[k("kdw")])
                S.op('act', lambda e: e.activation(out=cvec[:], in_=G[:], func=AF.Exp), r=[k("G")], w=[k("cvec")])
                S.op('dve', lambda e: e.tensor_tensor(out=cvec[:], in0=nbet[:], in1=cvec[:], op=ALU.mult),
                     r=[k("nbet"), k("cvec")], w=[k("cvec")])
                yield

            def setup(cs, B, sm, kbg):
                bet, nbet, G, kdw, cvec = (sm[n] for n in ("bet", "nbet", "G", "kdw", "cvec"))
                k = lambda n: sm[n].name
                bk = lambda n: B[n].name
                S.op('dve', lambda e: e.tensor_tensor(out=h3(X1[:]), in0=bh(identf), in1=bn(G[:]), op=ALU.mult),
                     r=['identf', k("G")], w=['n_X1'])
                pdG, pkG = dbank()
                S.mm([lambda e: e.matmul(pdG[:, 0:512], lhsT=onesF[:], rhs=X1[:, 0:512], start=True, stop=True),
                      lambda e: e.matmul(pdG[:, 512:1024], lhsT=onesF[:], rhs=X1[:, 512:1024], start=True, stop=True)],
                     r=['n_X1', 'n_ones'], w=pkG)
                pbk, pkk = bank()
                pbkb = pbk.bitcast(BF16)
                S.mm([lambda e, h=h: e.transpose(pbkb[:, h * 128:(h + 1) * 128], qk[:, 8 + h, cs], identb[:, :])
                      for h in range(8)], r=qkk[8:16] + ['identb'], w=[pkk])
                S.op('dve', lambda e: e.tensor_tensor(out=h3(B["kdec"][:]), in0=h3(pbkb), in1=bn(kdw[:]), op=ALU.mult),
                     r=[pkk, k("kdw")], w=[bk("kdec")])
                S.op('dve', lambda e: e.tensor_tensor(out=h3(kbg[:]), in0=h3(pbkb), in1=bn(cvec[:]), op=ALU.mult),
                     r=[pkk, k("cvec")], w=[kbg.name])
                pbv, pkv = bank()
                pbvb = pbv.bitcast(BF16)
                S.mm([lambda e, h=h: e.transpose(pbvb[:, h * 128:(h + 1) * 128], vT[:, h, cs], identb[:, :])
                      for h in range(8)], r=vTk + ['identb'], w=[pkv])
                S.op('dve', lambda e: e.tensor_tensor(out=h3(B["vb"][:]), in0=h3(pbvb), in1=bn(bet[:]), op=ALU.mult),
                     r=[pkv, k("bet")], w=[bk("vb")])
                yield
                S.op('dve', lambda e: e.tensor_tensor(out=h3(X1[:]), in0=h3(pdG), in1=bn(G[:]), op=ALU.subtract),
                     r=pkG + [k("G")], w=['n_X1'])
                S.op('act', lambda e: e.activation(out=X4[:], in_=pdG, func=AF.Exp), r=pkG, w=['n_X4'])
                S.op('act', lambda e: e.activation(out=X1[:], in_=X1[:], func=AF.Abs), r=['n_X1'], w=['n_X1'])
                S.op('act', lambda e: e.activation(out=X1[:], in_=X1[:], func=AF.Exp, scale=-1.0), r=['n_X1'], w=['n_X1'])
                S.op('pool', lambda e: e.tensor_tensor(out=h3(B["qg"][:]), in0=qk[:, 0:8, cs], in1=h3(X4[:]), op=ALU.mult),
                     r=['n_X4'] + qkk[0:8], w=[bk("qg")])
                S.op('dve', lambda e: e.tensor_tensor(out=h3(X2[:]), in0=h3(X1[:]), in1=bh(mL), op=ALU.mult),
                     r=['n_X1', 'n_mL'], w=['n_X2'])
                S.op('dve', lambda e: e.tensor_tensor(out=h3(X2[:]), in0=h3(X2[:]), in1=bn(nbet[:]), op=ALU.mult),
                     r=['n_X2', k("nbet")], w=['n_X2'])
                S.op('pool', lambda e: e.tensor_tensor(out=h3(X4[:]), in0=h3(X1[:]), in1=bh(mU), op=ALU.mult),
                     r=['n_X1', 'n_mU'], w=['n_X4'])
                yield
                pdK, pkK = dbank()
                S.mm([lambda e, h=h: e.matmul(pdK[:, h * 128:(h + 1) * 128], lhsT=qk[:, 8 + h, cs], rhs=qk[:, 8 + h, cs],
                                              start=True, stop=True) for h in range(8)], r=qkk[8:16], w=pkK)
                S.op('dve', lambda e: e.tensor_tensor(out=NtS[:], in0=pdK, in1=X2[:], op=ALU.mult), r=pkK + ['n_X2'],
                     w=[('n_NtS', 0), ('n_NtS', 1)])
                pdQ, pkQ = dbank()
                S.mm([lambda e, h=h: e.matmul(pdQ[:, h * 128:(h + 1) * 128], lhsT=qk[:, 8 + h, cs], rhs=qk[:, h, cs],
                                              start=True, stop=True) for h in range(8)], r=qkk, w=pkQ)
                S.op('dve', lambda e: e.tensor_tensor(out=B["attnT"][:], in0=pdQ, in1=X4[:], op=ALU.mult),
                     r=pkQ + ['n_X4'], w=[bk("attnT")])
                yield
                pdM, pkM = dbank()
                S.mm([lambda e, h=h: e.matmul(pdM[:, h * 128:(h + 1) * 128], lhsT=NtS[:, h * 128:(h + 1) * 128],
                                              rhs=identr_r, start=True, stop=True) for h in range(8)],
                     r=[('n_NtS', 0), ('n_NtS', 1), 'identb'], w=pkM)
                S.op('act', lambda e: e.activation(out=MPS[:, :, 0:128], in_=h3(pdM), func=AF.Copy), r=pkM,
                     w=[('n_MPS', 0), ('n_MPS', 1)])
                S.op('pool', lambda e: e.tensor_copy(out=MPS[:, :, 128:256], in_=bh(identf)),
                     r=['identf', ('n_MPS', 0), ('n_MPS', 1)], w=[('n_MPS', 0), ('n_MPS', 1)])
                yield

            def levels(B, kbg):
                bk = lambda n: B[n].name
                for l in range(6):
                    if l == 0:
                        Nc, Mc, Mcf, sN, sM = NtS[:], MPS[:], MPS, 'n_NtS', 'n_MPS'
                    else:
                        Nc, Mc, Mcf, sN, sM = Ntr[(l - 1) % 2], MPr[(l - 1) % 2], MP[(l - 1) % 2], 'n_Nt%d' % ((l - 1) % 2), \
                            'n_MP%d' % ((l - 1) % 2)
                    Nn, Mn, dN, dM = Ntr[l % 2], MPr[l % 2], 'n_Nt%d' % (l % 2), 'n_MP%d' % (l % 2)
                    if l < 5:
                        pdN, pkN = dbank()
                    else:
                        pdP, pkP = dbank()
                    for half in range(2):
                        hs_ = slice(4 * half, 4 * half + 4)
                        kNc, kMc = (sN, half), (sM, half)
                        kNn, kMn = (dN, half), (dM, half)
                        if l < 5:
                            pq, pkq = dbank()
                            S.mm([lambda e, hh=hh, pq=pq, Nc=Nc, Mc=Mc, half=half: e.matmul(
                                pq[:, hh * 256:(hh + 1) * 256], lhsT=Nc[:, (4 * half + hh) * 128:(4 * half + hh + 1) * 128],
                                rhs=Mc[:, 4 * half + hh, :], start=True, stop=True) for hh in range(4)],
                                r=[kNc, kMc], w=pkq)
                            S.mm([lambda e, hh=hh, pdN=pdN, Nc=Nc, Mc=Mc, half=half: e.matmul(
                                pdN[:, (4 * half + hh) * 128:(4 * half + hh + 1) * 128], lhsT=Mc[:, 4 * half + hh, 0:128],
                                rhs=Nc[:, (4 * half + hh) * 128:(4 * half + hh + 1) * 128], start=True, stop=True)
                                for hh in range(4)], r=[kNc, kMc], w=[pkN[half]])
                            pv = pq.rearrange("p (h n) -> p h n", h=4)
                            S.op('dve', lambda e, pv=pv, hs_=hs_, Mn=Mn: e.tensor_copy(out=Mn[:, hs_, 0:128],
                                                                                      in_=pv[:, :, 0:128]),
                                 r=pkq, w=[kMn])
                            S.op('dve', lambda e, pv=pv, hs_=hs_, Mn=Mn, Mcf=Mcf: e.tensor_tensor(
                                out=Mn[:, hs_, 128:256], in0=pv[:, :, 128:256], in1=Mcf[:, hs_, 128:256], op=ALU.add),
                                r=pkq + [kMc, kMn], w=[kMn])
                            S.op('act', lambda e, pdN=pdN, Nn=Nn, half=half: e.activation(
                                out=Nn[:, half * 512:(half + 1) * 512], in_=pdN[:, half * 512:(half + 1) * 512], func=AF.Copy),
                                r=[pkN[half]], w=[kNn])
                        else:
                            S.mm([lambda e, hh=hh, pdP=pdP, Nc=Nc, Mc=Mc, half=half: e.matmul(
                                pdP[:, (4 * half + hh) * 128:(4 * half + hh + 1) * 128],
                                lhsT=Nc[:, (4 * half + hh) * 128:(4 * half + hh + 1) * 128], rhs=Mc[:, 4 * half + hh, 128:256],
                                start=True, stop=True) for hh in range(4)], r=[kNc, kMc], w=[pkP[half]])
                            S.op('dve', lambda e, pdP=pdP, Mcf=Mcf, hs_=hs_, half=half: e.tensor_tensor(
                                out=h3(B["TT"][:])[:, hs_, :], in0=h3(pdP)[:, hs_, :], in1=Mcf[:, hs_, 128:256], op=ALU.add),
                                r=[pkP[half], kMc], w=[(bk("TT"), half)])
                    yield
                pdC, pkC = dbank()
                S.mm([lambda e, h=h: e.matmul(pdC[:, h * 128:(h + 1) * 128], lhsT=kbg[:, h * 128:(h + 1) * 128],
                                              rhs=B["TT"][:, h * 128:(h + 1) * 128], start=True, stop=True)
                      for h in range(8)], r=[kbg.name, (bk("TT"), 0), (bk("TT"), 1)], w=pkC)
                S.op('act', lambda e: e.activation(out=B["nkcT"][:], in_=pdC, func=AF.Copy), r=pkC, w=[bk("nkcT")])
                yield

            def chain(m, B, sm, is_last):
                c0 = 16 + 128 * m
                bk = lambda n: B[n].name
                TT3, vb3, nk3, qg3, at3, kd3 = (h3(B[n][:]) for n in ("TT", "vb", "nkcT", "qg", "attnT", "kdec"))
                for X in range(2):
                    pr = slice(64 * X, 64 * X + 64)
                    cX = slice(c0 + 64 * X, c0 + 64 * X + 64)
                    vnk = [('n_vn', X, 0), ('n_vn', X, 1)]
                    pd1, pk1 = dbank()
                    fns = []
                    for h in range(8):
                        fns.append(lambda e, h=h, pd1=pd1, pr=pr: e.matmul(
                            pd1[pr, h * 128:(h + 1) * 128], lhsT=TT3[pr, h, pr], rhs=vb3[pr, h, :], start=(h % 4 == 0),
                            stop=False, skip_group_check=True))
                    for h in range(8):
                        fns.append(lambda e, h=h, pd1=pd1, pr=pr: e.matmul(
                            pd1[pr, h * 128:(h + 1) * 128], lhsT=nk3[:, h, pr], rhs=Sb[:, h, :], start=False, stop=True,
                            skip_group_check=True))
                    S.mm(fns, r=[(bk("TT"), 0), (bk("TT"), 1), bk("vb"), bk("nkcT"), 'Sb'], w=pk1)
                    S.op('act', lambda e, pd1=pd1, pr=pr: e.activation(out=vn[pr, 0:512], in_=pd1[pr, 0:512], func=AF.Copy),
                         r=[pk1[0]], w=[vnk[0]])
                    S.op('dve', lambda e, pd1=pd1, pr=pr: e.tensor_copy(out=vn[pr, 512:1024], in_=pd1[pr, 512:1024]),
                         r=[pk1[1]], w=[vnk[1]])
                    yield
                    pbo, pko = bank()
                    fns = []
                    for h in range(8):
                        fns.append(lambda e, h=h, pbo=pbo, pr=pr: e.matmul(
                            pbo[:, h * 64:(h + 1) * 64], lhsT=Sb[:, h, :], rhs=qg3[:, h, pr], start=(h == 0), stop=False,
                            skip_group_check=True))
                    for h in range(8):
                        fns.append(lambda e, h=h, pbo=pbo, pr=pr: e.matmul(
                            pbo[:, h * 64:(h + 1) * 64], lhsT=vn[pr, h * 128:(h + 1) * 128], rhs=at3[pr, h, pr],
                            start=False, stop=True, skip_group_check=True))
                    S.mm(fns, r=['Sb', bk("qg"), bk("attnT")] + vnk, w=[pko])
                    S.op('act', lambda e, pbo=pbo, cX=cX: e.activation(
                        out=vT[:, :, cX], in_=pbo.rearrange("p (h n) -> p h n", h=8), func=AF.Copy), r=[pko], w=vTk)
                    pd3, pk3 = dbank()
                    S.mm([lambda e, h=h, pd3=pd3, pr=pr: e.matmul(
                        pd3[:, h * 128:(h + 1) * 128], lhsT=kd3[pr, h, :], rhs=vn[pr, h * 128:(h + 1) * 128], start=True,
                        stop=True) for h in range(8)], r=[bk("kdec")] + vnk, w=pk3)
                    S.op('pool', lambda e, X=X: e.tensor_tensor(out=Sf[:], in0=Sf[:], in1=bn(sm["egl2"][:, 8 * X:8 * X + 8]),
                                                                op=ALU.mult), r=['Sf', sm["egl2"].name], w=['Sf'])
                    S.op('dve', lambda e, pd3=pd3: e.tensor_tensor(out=Sb[:], in0=Sf[:], in1=h3(pd3), op=ALU.add),
                         r=['Sf'] + pk3, w=['Sb'])
                    S.op('dve', lambda e, pd3=pd3: e.tensor_tensor(out=Sf[:], in0=Sf[:], in1=h3(pd3), op=ALU.add),
                         r=['Sf'] + pk3, w=['Sf'])
                    yield
                if is_last:
                    S.dma(s_out[0].rearrange("h k v -> k h v"), Sf[:], r=['Sf'])

            def drain(g):
                for _ in g:
                    pass

            def step(g):
                return g is not None and next(g, 'END') != 'END'

            tcs = lambda m: slice(16 + 128 * m, 144 + 128 * m)
            for m in range(min(3, NPAIR)):
                drain(small(tcs(m), sms[m]))
            drain(setup(tcs(0), Bs[0], sms[0], kbgs[0]))
            drain(levels(Bs[0], kbgs[0]))
            if NPAIR > 1:
                drain(setup(tcs(1), Bs[1], sms[1], kbgs[1]))
            for m in range(NPAIR):
                gc = chain(m, Bs[m % 2], sms[m % 4], m == NPAIR - 1)
                gl = levels(Bs[(m + 1) % 2], kbgs[(m + 1) % 2]) if m + 1 < NPAIR else None
                gs = setup(tcs(m + 2), Bs[m % 2], sms[(m + 2) % 4], kbgs[m % 2]) if m + 2 < NPAIR else None
                gsm = small(tcs(m + 3), sms[(m + 3) % 4]) if m + 3 < NPAIR else None
                while True:
                    a = step(gl)
                    b_ = step(gc)
                    if not b_:
                        break
                while True:
                    a = step(gl)
                    b_ = step(gs)
                    c_ = step(gsm)
                    if not (a or b_ or c_):
                        break
            if os.environ.get("OLD_S", "0") == "1":
                wk.close()
                end_phase()
                return
            S.barrier()
            for t, key in ((mU, 'n_mU'), (mL, 'n_mL')):
                S.op('pool', lambda e, t=t: e.memset(t[:], 1.0), w=[key])
                S.op('pool', lambda e, t=t: e.affine_select(out=t[:], in_=t[:], pattern=[[-4, 32], [0, 4]],
                                                            compare_op=ALU.is_ge, fill=0.0, base=0, channel_multiplier=1),
                     r=[key], w=[key])
                S.op('pool', lambda e, t=t: e.affine_select(out=t[:], in_=t[:], pattern=[[4, 32], [0, 4]],
                                                            compare_op=ALU.is_ge, fill=0.0, base=3, channel_multiplier=-1),
                     r=[key], w=[key])
            S.op('pool', lambda e: e.affine_select(out=mU[:], in_=mU[:], pattern=[[4, 32], [1, 4]], compare_op=ALU.is_ge,
                                                   fill=0.0, base=0, channel_multiplier=-1), r=['n_mU'], w=['n_mU'])
            S.op('pool', lambda e: e.affine_select(out=mL[:], in_=mL[:], pattern=[[-4, 32], [-1, 4]], compare_op=ALU.is_gt,
                                                   fill=0.0, base=0, channel_multiplier=1), r=['n_mL'], w=['n_mL'])
            B = Bs[0]
            drain(small(slice(2000, 2128), sms[0]))
            drain(setup(slice(2000, 2128), B, sms[0], kbgs[0]))
            drain(levels(B, kbgs[0]))
            S.barrier()
            wk.close()
            ps_ = slice(64, 128)
            TT3, vb3, nk3, qg3, at3 = (h3(B[n][:]) for n in ("TT", "vb", "nkcT", "qg", "attnT"))
            nkMs = [sbl("s_nkM%d" % i, [128, 2, 8, 64], BF16) for i in range(2)]
            kdM = sbl("s_kdM", [128, 2, 1024], BF16)
            Sgs = [sbl("s_Sg%d" % i, [128, 2, 8, 128], F32) for i in range(2)]
            Sbgs = [sbl("s_Sbg%d" % i, [128, 2, 8, 128], BF16) for i in range(2)]
            rowm = sbl("s_rowm", [128, 16], F32)
            Rt = sbl("s_R", [128, 16, 8], F32)
            egs = sbl("s_egs", [128, 16, 8], F32)
            g = sms[0]["g"]
            S.op('pool', lambda e: e.memset(rowm[:], 1.0), w=['s_rowm'])
            S.op('pool', lambda e: e.affine_select(out=rowm[:], in_=rowm[:], pattern=[[-4, 16]], compare_op=ALU.is_ge,
                                                   fill=0.0, base=-64, channel_multiplier=1), r=['s_rowm'], w=['s_rowm'])
            S.op('pool', lambda e: e.affine_select(out=rowm[:], in_=rowm[:], pattern=[[4, 16]], compare_op=ALU.is_ge,
                                                   fill=0.0, base=67, channel_multiplier=-1), r=['s_rowm'], w=['s_rowm'])
            S.op('dve', lambda e: e.tensor_tensor(out=Rt[:], in0=rowm[:, :].unsqueeze(2).to_broadcast([128, 16, 8]),
                                                  in1=g[:, :].unsqueeze(1).to_broadcast([128, 16, 8]), op=ALU.mult),
                 r=['s_rowm', sms[0]["g"].name], w=['s_R'])
            pE, pkE = bank()
            S.mm([lambda e: e.matmul(pE[:, 0:128], lhsT=onesF[:], rhs=Rt[:].rearrange("p b h -> p (b h)"), start=True,
                                     stop=True)], r=['s_R', 'n_ones'], w=[pkE])
            S.op('act', lambda e: e.activation(out=egs[:].rearrange("p b h -> p (b h)"), in_=pE[:, 0:128], func=AF.Exp),
                 r=[pkE], w=['s_egs'])
            pdv, pkv_ = dbank()
            pbo, pko = bank()
            S.mm([lambda e, h=h: e.matmul(pdv[ps_, h * 128:(h + 1) * 128], lhsT=TT3[ps_, h, ps_], rhs=vb3[ps_, h, :],
                                          start=(h % 4 == 0), stop=False, skip_group_check=True) for h in range(8)],
                 r=[(B["TT"].name, 0), (B["TT"].name, 1), B["vb"].name], w=pkv_)

            def samp_pass1(gi):
                Sg, Sbg, nkM = Sgs[gi % 2], Sbgs[gi % 2], nkMs[gi % 2]
                S.dma(Sg[:], s0[2 * gi:2 * gi + 2].rearrange("b h k v -> k b h v"), w=[Sg.name])
                S.op('act', lambda e: e.activation(out=Sbg[:], in_=Sg[:], func=AF.Copy), r=[Sg.name], w=[Sbg.name])
                S.op('pool', lambda e: e.memset(nkM[:], 0.0), w=[nkM.name])
                for bi in range(2):
                    b = 2 * gi + bi
                    S.op('pool', lambda e, b=b, bi=bi: e.tensor_copy(out=nkM[:, bi, :, 4 * b:4 * b + 4],
                                                                     in_=nk3[:, :, 64 + 4 * b:68 + 4 * b]),
                         r=[B["nkcT"].name, nkM.name], w=[nkM.name])
                fns = []
                for bi in range(2):
                    b = 2 * gi + bi
                    for h in range(8):
                        fns.append(lambda e, b=b, bi=bi, h=h: e.matmul(
                            pdv[ps_, h * 128:(h + 1) * 128], lhsT=nkM[:, bi, h, :], rhs=Sbg[:, bi, h, :], start=False,
                            stop=(b == 15), skip_group_check=True))
                        fns.append(lambda e, b=b, bi=bi, h=h: e.matmul(
                            pbo[:, h * 64 + 4 * b:h * 64 + 4 * b + 4], lhsT=Sbg[:, bi, h, :],
                            rhs=qg3[:, h, 64 + 4 * b:68 + 4 * b], start=(b == 0 and h == 0), stop=False,
                            skip_group_check=True))
                S.mm(fns, r=[nkM.name, Sbg.name, B["qg"].name], w=pkv_ + [pko])

            for gi in range(8):
                samp_pass1(gi)
            S.op('act', lambda e: e.activation(out=vn[ps_, 0:512], in_=pdv[ps_, 0:512], func=AF.Copy), r=[pkv_[0]],
                 w=['s_vn0'])
            S.op('dve', lambda e: e.tensor_copy(out=vn[ps_, 512:1024], in_=pdv[ps_, 512:1024]), r=[pkv_[1]], w=['s_vn1'])
            S.mm([lambda e, h=h: e.matmul(pbo[:, h * 64:(h + 1) * 64], lhsT=vn[ps_, h * 128:(h + 1) * 128],
                                          rhs=at3[ps_, h, ps_], start=False, stop=True,
                                          skip_group_check=True) for h in range(8)],
                 r=['s_vn0', 's_vn1', B["attnT"].name], w=[pko])
            S.op('act', lambda e: e.activation(out=vT[:, :, NP:NT], in_=pbo.rearrange("p (h n) -> p h n", h=8),
                                               func=AF.Copy), r=[pko], w=vTk)

            def samp_pass2(gi):
                Sg = Sgs[gi % 2]
                S.dma(Sg[:], s0[2 * gi:2 * gi + 2].rearrange("b h k v -> k b h v"), w=[Sg.name])
                pss = []
                for bi in range(2):
                    b = 2 * gi + bi
                    S.op('dve', lambda e, b=b, bi=bi: e.tensor_scalar(out=kdM[ps_, bi, :], in0=B["kdec"][ps_, :],
                                                                      scalar1=rowm[ps_, b:b + 1], scalar2=None,
                                                                      op0=ALU.mult),
                         r=[B["kdec"].name, 's_rowm'], w=[('s_kdM', bi)])
                    pS, pkS = dbank()
                    S.mm([lambda e, bi=bi, h=h, pS=pS: e.matmul(pS[:, h * 128:(h + 1) * 128],
                                                                lhsT=kdM[ps_, bi, h * 128:(h + 1) * 128],
                                                                rhs=vn[ps_, h * 128:(h + 1) * 128], start=True, stop=True)
                          for h in range(8)], r=[('s_kdM', bi), 's_vn0', 's_vn1'], w=pkS)
                    pss.append((pS, pkS))
                S.op('pool', lambda e: e.tensor_tensor(
                    out=Sg[:], in0=Sg[:],
                    in1=egs[:, 2 * gi:2 * gi + 2, :].unsqueeze(3).to_broadcast([128, 2, 8, 128]), op=ALU.mult),
                    r=[Sg.name, 's_egs'], w=[Sg.name])
                for bi, (pS, pkS) in enumerate(pss):
                    S.op('dve', lambda e, bi=bi, pS=pS: e.tensor_tensor(out=Sg[:, bi, :, :], in0=Sg[:, bi, :, :],
                                                                        in1=pS.rearrange("p (h n) -> p h n", h=8),
                                                                        op=ALU.add), r=[Sg.name] + pkS, w=[Sg.name])
                S.dma(s_out[1 + 2 * gi:3 + 2 * gi].rearrange("b h k v -> k b h v"), Sg[:], r=[Sg.name])

            for gi in range(8):
                samp_pass2(gi)
            end_phase()

        NEWPATH = os.environ.get("OLD_B", "0") != "1"
        if not NEWPATH:
            old_path(list(range(len(_chunks()))), "a")
        else:
            old_path([0], "a")
            new_pairs()
            if os.environ.get("OLD_S", "0") == "1":
                old_path(list(range(33, 49)), "c")
        if debug:
            dbg['oT'] = dout("dbg_oT", [128, 8, NT], BF16)
            S.dma(dbg['oT'], vT[:], r=vTk)
        end_phase()

    if upto >= 3:
        ws2 = WStream([(w_in, OFF_ZB + h * 128) for h in range(8)], 3, 4, 2)
        b2 = [{n: sbl("b2_%s%d" % (n, i), [128, 512], dt) for n, dt in (("sz", BF16), ("sq", BF16), ("ln", F32),
                                                                        ("tmp", F32))} for i in range(4)]
        b2n = [0]

        def b2_norm(h, ti, t0, t1):
            n = t1 - t0
            B = b2[b2n[0] % 4]
            b2n[0] += 1
            sq, ln = B["sq"], B["ln"]
            S.op('act', lambda e: e.activation(out=sq[:, 0:n], in_=vT[:, h, t0:t1], func=AF.Square),
                 r=[('vT', h)], w=[sq.name])
            pss, pks = bank()
            S.mm([lambda e: e.matmul(pss[:, 0:n], lhsT=onesb[:], rhs=sq[:, 0:n], start=True, stop=True)],
                 r=[sq.name, 'onesb'], w=[pks])
            S.op('act', lambda e: e.activation(out=ln[:, 0:n], in_=pss[:, 0:n], func=AF.Ln, scale=1.0 / 128, bias=EPS),
                 r=[pks], w=[ln.name])
            S.op('act', lambda e: e.activation(out=ln[:, 0:n], in_=ln[:, 0:n], func=AF.Exp, scale=-0.5), r=[ln.name],
                 w=[ln.name])
            S.op('dve', lambda e: e.scalar_tensor_tensor(out=vT[:, h, t0:t1], in0=vT[:, h, t0:t1], scalar=gnorm[:, 0:1],
                                                         in1=ln[:, 0:n], op0=ALU.mult, op1=ALU.mult),
                 r=[('vT', h), ln.name, 'gnorm_s'], w=[('vT', h)])

        def b2_gate(h, wt, wk, ti, t0, t1):
            n = t1 - t0
            sz = b2[b2n[0] % 4]["sz"]
            b2n[0] += 1
            pz, pkz = proj(wt, wk, xnT, 'xnT', t0, t1)
            S.op('act', lambda e: e.activation(out=sz[:, 0:n], in_=pz[:, 0:n], func=AF.Silu), r=[pkz], w=[sz.name])
            S.op('dve', lambda e: e.tensor_tensor(out=vT[:, h, t0:t1], in0=vT[:, h, t0:t1], in1=sz[:, 0:n], op=ALU.mult),
                 r=[('vT', h), sz.name], w=[('vT', h)])

        ws2.get(0)
        for h in range(8):
            for ti, (t0, t1) in enumerate(TBS):
                b2_norm(h, ti, t0, t1)
        for h in range(8):
            wt, wk = ws2.get(h)
            for ti, (t0, t1) in enumerate(TBS):
                b2_gate(h, wt, wk, ti, t0, t1)
        if debug:
            dbg['yb'] = dout("dbg_yb", [128, 8, NT], BF16)
            S.dma(dbg['yb'], vT[:], r=[('vT', h) for h in range(8)])
        end_phase()

    if upto >= 4:
        reqs = []
        for cc in range(8):
            reqs += [(w_in, OFF_H + cc * 128), (w_in, OFF_C + cc * 128), (w_in, OFF_B + cc * 128), (w_in, OFF_Z + cc * 128)]
        ws3 = WStream(reqs, 4, 8, 4)
        pend = []
        ubuf = [sbl("ubuf%d" % i, [128, 2 + NP], F32) for i in range(2)]
        usb = [sbl("us%d" % i, [128, NSQ, 6], F32) for i in range(2)]
        a2 = [{n: sbl("a2_%s%d" % (n, i), [128, 512], F32) for n in ("hs", "cv", "sz", "tmp")} for i in range(2)]
        for i in range(2):
            S.op('pool', lambda e, i=i: e.memset(ubuf[i][:, 0:2], 0.0), w=[ubuf[i].name])

        def do_a2(cc):
            ub, us = ubuf[cc % 2], usb[cc % 2]
            wh, kh = ws3.get(4 * cc)
            wc, kcw = ws3.get(4 * cc + 1)
            wb_, kb = ws3.get(4 * cc + 2)
            wz, kz = ws3.get(4 * cc + 3)
            for f in pend:
                f()
            pend.clear()
            S.dma(us[:, :, 0:2], sca[:, cc, :, :], w=[us.name])
            cw = lambda j: cwa[:, cc * 3 + j:cc * 3 + j + 1]

            def do_tb(ti, t0, t1):
                n = t1 - t0
                A = a2[ti % 2]
                hs, cv, sz, tmp = A["hs"], A["cv"], A["sz"], A["tmp"]
                p_h, k_h = proj(wh, kh, xnT, 'xnT', t0, t1)
                p_c, k_c = proj(wc, kcw, xnT, 'xnT', t0, t1)
                p_b, k_b = proj(wb_, kb, xnT, 'xnT', t0, t1)
                p_z, k_z = proj(wz, kz, xnT, 'xnT', t0, t1)
                S.op('act', lambda e: e.activation(out=hs[:, 0:n], in_=p_h[:, 0:n], func=AF.Copy), r=[k_h], w=[hs.name])
                npr = min(t1, NP) - t0
                S.op('dve', lambda e: e.tensor_tensor(out=ub[:, 2 + t0:2 + t0 + npr], in0=p_c[:, 0:npr], in1=hs[:, 0:npr],
                                                      op=ALU.mult), r=[k_c, hs.name], w=[ub.name])
                S.op('dve', lambda e: e.tensor_scalar(out=cv[:, 0:npr], in0=ub[:, t0:t0 + npr], scalar1=cw(0), scalar2=None,
                                                      op0=ALU.mult), r=[ub.name, 'cwa_s'], w=[cv.name])
                for j in (1, 2):
                    S.op('dve', lambda e, j=j: e.scalar_tensor_tensor(
                        out=cv[:, 0:npr], in0=ub[:, t0 + j:t0 + j + npr], scalar=cw(j), in1=cv[:, 0:npr], op0=ALU.mult,
                        op1=ALU.add), r=[ub.name, cv.name], w=[cv.name])
                if t1 > NP:
                    v4 = lambda ap: ap.rearrange("p (b t) -> p b t", t=4)
                    S.op('dve', lambda e: e.tensor_tensor(out=us[:, :, 2:6], in0=v4(p_c[:, npr:n]), in1=v4(hs[:, npr:n]),
                                                          op=ALU.mult), r=[k_c, hs.name], w=[us.name])
                    S.op('dve', lambda e: e.tensor_scalar(out=v4(cv[:, npr:n]), in0=us[:, :, 0:4], scalar1=cw(0),
                                                          scalar2=None, op0=ALU.mult), r=[us.name, 'cwa_s'], w=[cv.name])
                    for j in (1, 2):
                        S.op('dve', lambda e, j=j: e.scalar_tensor_tensor(
                            out=v4(cv[:, npr:n]), in0=us[:, :, j:j + 4], scalar=cw(j), in1=v4(cv[:, npr:n]), op0=ALU.mult,
                            op1=ALU.add), r=[us.name, cv.name], w=[cv.name])
                S.op('act', lambda e: e.activation(out=sz[:, 0:n], in_=p_z[:, 0:n], func=AF.Silu), r=[k_z], w=[sz.name])
                S.op('dve', lambda e: e.tensor_tensor(out=tmp[:, 0:n], in0=p_b[:, 0:n], in1=cv[:, 0:n], op=ALU.mult),
                     r=[k_b, cv.name], w=[tmp.name])
                S.op('pool', lambda e: e.tensor_tensor(out=qk[:, cc, t0:t1], in0=tmp[:, 0:n], in1=sz[:, 0:n], op=ALU.mult),
                     r=[tmp.name, sz.name], w=[('qk', cc)])

            for ti, (t0, t1) in enumerate(TBS):
                do_tb(ti, t0, t1)
            pend.append(lambda: S.dma(oca[:, cc, 0, :], ub[:, NP:NP + 2], r=[ub.name]))
            pend.append(lambda: S.dma(oca[:, cc, 1:1 + NSQ, :], us[:, :, 4:6], r=[us.name]))

        for cc in range(8):
            do_a2(cc)
        for f in pend:
            f()
        pend.clear()
        if debug:
            dbg['ya'] = dout("dbg_ya", [128, 8, NT], BF16)
            S.dma(dbg['ya'], qk[:, 0:8, :], r=[('qk', i) for i in range(8)])
        end_phase()

    if upto >= 5:
        reqs = []
        for oc in range(8):
            reqs += [(w_a_out, oc * 128), (w_b_out, oc * 128), (w_in, OFF_GA + oc * 128), (w_in, OFF_GB + oc * 128)]
        ws4 = WStream(reqs, 4, 8, 4)
        c2 = [{n: sbl("c_%s%d" % (n, i), [128, 512], F32) for n in ("sa", "sb", "m1", "m2")} for i in range(2)]

        def do_c(oc):
            wa, ka = ws4.get(4 * oc)
            wb_, kb = ws4.get(4 * oc + 1)
            wga, kga = ws4.get(4 * oc + 2)
            wgb, kgb = ws4.get(4 * oc + 3)

            def do_tb(ti, t0, t1):
                n = t1 - t0
                C = c2[ti % 2]
                sa, sb_, m1, m2 = C["sa"], C["sb"], C["m1"], C["m2"]
                p_a, k_a = proj(wa, ka, qk, None, t0, t1, extra_r=[('qk', i) for i in range(8)])
                p_ga, k_ga = proj(wga, kga, xnT, 'xnT', t0, t1)
                p_b, k_b = proj(wb_, kb, vT, None, t0, t1, extra_r=[('vT', i) for i in range(8)])
                p_gb, k_gb = proj(wgb, kgb, xnT, 'xnT', t0, t1)
                S.op('act', lambda e: e.activation(out=sa[:, 0:n], in_=p_ga[:, 0:n], func=AF.Sigmoid,
                                                   bias=bgate[:, oc:oc + 1]), r=[k_ga, 'bgate_s'], w=[sa.name])
                S.op('act', lambda e: e.activation(out=sb_[:, 0:n], in_=p_gb[:, 0:n], func=AF.Sigmoid,
                                                   bias=bgate[:, 8 + oc:9 + oc]), r=[k_gb, 'bgate_s'], w=[sb_.name])
                S.op('dve', lambda e: e.tensor_tensor(out=m1[:, 0:n], in0=p_a[:, 0:n], in1=sa[:, 0:n], op=ALU.mult),
                     r=[k_a, sa.name], w=[m1.name])
                S.op('dve', lambda e: e.tensor_tensor(out=m2[:, 0:n], in0=p_b[:, 0:n], in1=sb_[:, 0:n], op=ALU.mult),
                     r=[k_b, sb_.name], w=[m2.name])
                S.op('pool', lambda e: e.tensor_tensor(out=qk[:, 8 + oc, t0:t1], in0=m1[:, 0:n], in1=m2[:, 0:n], op=ALU.add),
                     r=[m1.name, m2.name], w=[('qk', 8 + oc)])

            for ti, (t0, t1) in enumerate(TBS):
                do_tb(ti, t0, t1)

        for oc in range(8):
            do_c(oc)
        if debug:
            dbg['mg'] = dout("dbg_mg", [128, 8, NT], BF16)
            S.dma(dbg['mg'], qk[:, 8:16, :], r=[('qk', 8 + i) for i in range(8)])
        end_phase()

    if upto >= 6:
        wst[:] = [sbl("wstf%d" % i, [128, 8, 128], F32) for i in range(3)]
        xt2 = [sbl("fx%d" % i, [128, D], F32) for i in range(2)]
        yt2 = [sbl("fy%d" % i, [128, D], F32) for i in range(2)]
        junk2 = sbl("fjunk", [128, D], BF16)
        npost = sbl("npost_s", [128, D], F32)
        S.dma(npost[:], npost_d, w=['npost_s'])
        fsq = [sbl("fsq%d" % i, [128, 1], F32) for i in range(2)]
        wo = xnT
        for j in range(8):
            st = wst[j % 3]
            S.dma(st[:], w_o[:, j * 128:(j + 1) * 128].rearrange("(kc p) f -> p kc f", p=128), w=[st.name])
            S.op('pool', lambda e, st=st, j=j: e.tensor_copy(out=wo[:, :, j * 128:(j + 1) * 128], in_=st[:]),
                 r=[st.name], w=['xnT'])
        ftiles = [(16 + 128 * i, 128) for i in range(16)] + [(NP, NS)]
        mgk = [('qk', 8 + i) for i in range(8)]

        def do_final(i, r0, n):
            x_, y_, q_ = xt2[i % 2], yt2[i % 2], fsq[i % 2]
            S.dma(x_[0:n, :], x_all[r0:r0 + n, :], w=[x_.name])
            pd, pks = dbank()
            fns = []
            for half in range(2):
                for kc in range(8):
                    fns.append(lambda e, half=half, kc=kc: e.matmul(
                        pd[0:n, half * 512:(half + 1) * 512], lhsT=qk[:, 8 + kc, r0:r0 + n],
                        rhs=wo[:, kc, half * 512:(half + 1) * 512], start=(kc == 0), stop=(kc == 7)))
            S.mm(fns, r=mgk + ['xnT'], w=pks)
            S.op('act', lambda e: e.activation(out=junk2[0:n, :], in_=pd[0:n, :], func=AF.Square, accum_out=q_[0:n, :]),
                 r=pks, w=['fjunk', q_.name])
            S.op('act', lambda e: e.activation(out=q_[0:n, :], in_=q_[0:n, :], func=AF.Sqrt, scale=1.0 / D, bias=EPS),
                 r=[q_.name], w=[q_.name])
            S.op('dve', lambda e: e.reciprocal(out=q_[0:n, :], in_=q_[0:n, :]), r=[q_.name], w=[q_.name])
            S.op('dve', lambda e: e.scalar_tensor_tensor(out=y_[0:n, :], in0=pd[0:n, :], scalar=q_[0:n, 0:1],
                                                         in1=npost[0:n, :], op0=ALU.mult, op1=ALU.mult),
                 r=pks + [q_.name, 'npost_s'], w=[y_.name])
            S.op('pool', lambda e: e.tensor_tensor(out=y_[0:n, :], in0=y_[0:n, :], in1=x_[0:n, :], op=ALU.add),
                 r=[y_.name, x_.name], w=[y_.name])
            S.dma(y_out[r0 - 16:r0 - 16 + n, :], y_[0:n, :], r=[y_.name])

        for i, (r0, n) in enumerate(ftiles):
            do_final(i, r0, n)

    S.finish()
    ph[0].close()
    block = es.enter_context(nc.Block())
    S.emit(block)
    es.close()
    return nc, dbg


def prep_inputs(inp, c):
    f = lambda a: np.ascontiguousarray(a, dtype=np.float32)
    xs = inp["x_sample"][16 * c:16 * c + 16].reshape(NS, D)
    x_all = np.concatenate([inp["meta"], inp["x_prompt"][c], xs], axis=0)
    scq = inp["state_conv_qkv"][0, 16 * c:16 * c + 16]
    scq = scq.reshape(16, 3, 24, 128).transpose(3, 2, 0, 1)
    sca = inp["state_conv_a"][0, 16 * c:16 * c + 16].reshape(16, 2, 8, 128).transpose(3, 2, 0, 1)
    m = {
        "x_all": f(x_all), "w_in": f(inp["w_in"][0]), "w_a_out": f(inp["w_a_out"][0]), "w_b_out": f(inp["w_b_out"][0]),
        "w_o": f(inp["w_o"][0]), "scq": f(scq), "sca": f(sca), "s0": f(inp["state_delta"][0, 16 * c:16 * c + 16]),
        "cwq": f(inp["conv_qkv_w"][0].reshape(4, 24, 128).transpose(2, 1, 0).reshape(128, 96)),
        "cwa": f(inp["conv_a_w"][0].reshape(3, 8, 128).transpose(2, 1, 0).reshape(128, 24)),
        "bgate": f(inp["b_gate"][0].reshape(16, 128).T), "gnorm": f(inp["gnorm_w"][0].reshape(128, 1)),
        "npre_bc": f(np.broadcast_to(inp["norm_pre"][0][None, :], (128, D))),
        "npost_bc": f(np.broadcast_to(inp["norm_post"][0][None, :], (128, D))),
        "alog_bc": f(np.broadcast_to(inp["a_log"][0][None, :], (128, 8))),
        "dtb_bc": f(np.broadcast_to(inp["dt_bias"][0][None, :], (128, 8))),
    }
    return m


def run(inputs, upto=99, debug=False, cores=NCORES):
    inp = {k: np.asarray(v) for k, v in inputs.items()}
    nc, dbg = build(upto=upto, debug=debug)
    in_maps = [prep_inputs(inp, c) for c in range(cores)]
    res = run_bass_kernel_spmd(nc, in_maps, core_ids=list(range(cores)))
    return res.results


def kernel(**inputs):
    rs = run(inputs)
    B = NCORES
    y_prompt = np.stack([r["y_out"][0:2048] for r in rs]).astype(np.float32)
    y_sample = np.concatenate([r["y_out"][2048:].reshape(NSQ, 4, D) for r in rs]).astype(np.float32)
    cq = [np.asarray(r["ocq"]).transpose(2, 3, 1, 0).reshape(1 + NSQ, 3, 3072) for r in rs]
    ca = [np.asarray(r["oca"]).transpose(2, 3, 1, 0).reshape(1 + NSQ, 2, 1024) for r in rs]
    so = [np.asarray(r["s_out"]) for r in rs]
    conv_a_prompt = np.stack([a[0] for a in ca])[None].astype(np.float32)
    conv_qkv_prompt = np.stack([a[0] for a in cq])[None].astype(np.float32)
    delta_prompt = np.stack([a[0] for a in so])[None].astype(np.float32)
    conv_a_sample = np.concatenate([a[1:] for a in ca])[None].astype(np.float32)
    conv_qkv_sample = np.concatenate([a[1:] for a in cq])[None].astype(np.float32)
    delta_sample = np.concatenate([a[1:] for a in so])[None].astype(np.float32)
    return (y_prompt, y_sample, conv_a_prompt, conv_qkv_prompt, delta_prompt,
            conv_a_sample, conv_qkv_sample, delta_sample)
```
